# Optimizing a Trainium2 kernel written in Bass

```python
import jax, jax.numpy as jnp
from jax import lax
import numpy as np

D_MODEL = 1024
BATCH = 4
SEQ = 4096
DEPTH = 1

GRID_W = 64
HEAD_DIM = 64
D_ATTN = D_MODEL // 2
N_Q_HEADS = D_ATTN // HEAD_DIM
N_KV_HEADS = 2
Q_PER_KV = N_Q_HEADS // N_KV_HEADS
D_KV = N_KV_HEADS * HEAD_DIM
Q_BLOCK = 128
ROPE_THETA = 10000.0
AXIS_DIM = HEAD_DIM // 2
N_FREQ = AXIS_DIM // 2
D_LRU = D_MODEL // 2
LRU_BLOCKS = 8
LRU_BLOCK_W = D_LRU // LRU_BLOCKS
LRU_C = 8.0
CONV_W = 4
CONV_PAD = (2, 1)
N_DIR = 2
D_MIX = D_ATTN + D_LRU
D_IN = D_ATTN + 2 * D_KV + 2 * D_LRU
D_FF = 4 * D_MODEL
D_PLE = 256
NORM_EPS = 1e-6

kernel_name = "hybrid_gqa_rglru_encoder_layer"


def rms_norm(x, g):
    xf = x.astype(jnp.float32)
    y = xf * lax.rsqrt(jnp.mean(xf * xf, axis=-1, keepdims=True) + NORM_EPS)
    return (y * g.astype(jnp.float32)).astype(x.dtype)


def _rotate(x, cos, sin):
    x1, x2 = jnp.split(x, 2, axis=-1)
    return jnp.concatenate([x1 * cos - x2 * sin, x2 * cos + x1 * sin], axis=-1)


def axial_rope(x, cos_r, sin_r, cos_c, sin_c):
    xf = x.astype(jnp.float32)
    xr, xc = jnp.split(xf, 2, axis=-1)
    out = jnp.concatenate([_rotate(xr, cos_r, sin_r), _rotate(xc, cos_c, sin_c)], axis=-1)
    return out.astype(x.dtype)


def grid_rope_tables(seq_len):
    rows = seq_len // GRID_W
    row = jnp.repeat(jnp.arange(rows, dtype=jnp.float32), GRID_W)
    col = jnp.tile(jnp.arange(GRID_W, dtype=jnp.float32), rows)
    inv_freq = ROPE_THETA ** (-jnp.arange(N_FREQ, dtype=jnp.float32) / N_FREQ)
    ang_r = row[:, None, None] * inv_freq
    ang_c = col[:, None, None] * inv_freq
    return jnp.cos(ang_r), jnp.sin(ang_r), jnp.cos(ang_c), jnp.sin(ang_c)


def block_attention(q, k, v):
    b, s = q.shape[0], q.shape[1]
    nb = s // Q_BLOCK
    scale = HEAD_DIM ** -0.5
    qb = q.reshape(b, nb, Q_BLOCK, N_KV_HEADS, Q_PER_KV, HEAD_DIM).transpose(1, 0, 2, 3, 4, 5)

    def attend(q_blk):
        sc = jnp.einsum('bqhgd,bkhd->bhgqk', q_blk, k, preferred_element_type=jnp.float32) * scale
        pr = jax.nn.softmax(sc, axis=-1)
        return jnp.einsum('bhgqk,bkhd->bqhgd', pr.astype(v.dtype), v)

    o = lax.map(attend, qb)
    return o.transpose(1, 0, 2, 3, 4, 5).reshape(b, s, D_ATTN)


def _lin_combine(c1, c2):
    a1, b1 = c1
    a2, b2 = c2
    return a1 * a2, a2 * b1 + b2


def bidirectional_rglru(xc, wa, ba, wx, bx, lam):
    b, s = xc.shape[0], xc.shape[1]
    xh = xc.reshape(b, s, LRU_BLOCKS, LRU_BLOCK_W)
    ga = jnp.einsum('bsnk,enkj->ebsnj', xh, wa).reshape(N_DIR, b, s, D_LRU) + ba[:, None, None, :]
    gx = jnp.einsum('bsnk,enkj->ebsnj', xh, wx).reshape(N_DIR, b, s, D_LRU) + bx[:, None, None, :]
    r = jax.nn.sigmoid(ga.astype(jnp.float32))
    i = jax.nn.sigmoid(gx.astype(jnp.float32))
    log_a = LRU_C * r * jax.nn.log_sigmoid(lam.astype(jnp.float32))[:, None, None, :]
    a = jnp.exp(log_a)
    u = jnp.sqrt(-jnp.expm1(2.0 * log_a)) * (i * xc.astype(jnp.float32)[None])
    _, h_fwd = lax.associative_scan(_lin_combine, (a[0], u[0]), axis=1)
    _, h_bwd = lax.associative_scan(_lin_combine, (a[1], u[1]), axis=1, reverse=True)
    return (h_fwd + h_bwd).astype(xc.dtype)


def setup_inputs(seed: int = 0) -> dict:
    key = jax.random.key(seed)
    ks = jax.random.split(key, 24)
    f32 = jnp.float32
    nrm = lambda k, shape, s: jax.random.normal(k, shape, f32) * s
    gain = lambda k, shape: 1.0 + 0.02 * jax.random.normal(k, shape, f32)
    u = jax.random.uniform(ks[12], (DEPTH, N_DIR, D_LRU), f32, 0.9, 0.999)
    a0 = u ** (1.0 / LRU_C)
    lam = jnp.log(a0) - jnp.log1p(-a0)
    return {
        "x": nrm(ks[0], (BATCH, SEQ, D_MODEL), 1.0),
        "p": nrm(ks[1], (DEPTH, BATCH, SEQ, D_PLE), 1.0),
        "mix_norm": gain(ks[2], (DEPTH, D_MODEL)),
        "w_in": nrm(ks[3], (DEPTH, D_MODEL, D_IN), D_MODEL ** -0.5),
        "q_norm": gain(ks[4], (DEPTH, HEAD_DIM)),
        "k_norm": gain(ks[5], (DEPTH, HEAD_DIM)),
        "conv_w": nrm(ks[6], (DEPTH, CONV_W, D_LRU), CONV_W ** -0.5),
        "conv_b": nrm(ks[7], (DEPTH, D_LRU), 0.01),
        "lru_wa": nrm(ks[8], (DEPTH, N_DIR, LRU_BLOCKS, LRU_BLOCK_W, LRU_BLOCK_W), LRU_BLOCK_W ** -0.5),
        "lru_ba": nrm(ks[9], (DEPTH, N_DIR, D_LRU), 0.01),
        "lru_wx": nrm(ks[10], (DEPTH, N_DIR, LRU_BLOCKS, LRU_BLOCK_W, LRU_BLOCK_W), LRU_BLOCK_W ** -0.5),
        "lru_bx": nrm(ks[11], (DEPTH, N_DIR, D_LRU), 0.01),
        "lru_lambda": lam,
        "attn_out_norm": gain(ks[13], (DEPTH, D_ATTN)),
        "lru_out_norm": gain(ks[14], (DEPTH, D_LRU)),
        "w_out": nrm(ks[15], (DEPTH, D_MIX, D_MODEL), D_MIX ** -0.5),
        "mlp_norm": gain(ks[16], (DEPTH, D_MODEL)),
        "w_up": nrm(ks[17], (DEPTH, D_MODEL, D_FF), D_MODEL ** -0.5),
        "w_down": nrm(ks[18], (DEPTH, D_FF, D_MODEL), D_FF ** -0.5),
        "ple_norm": gain(ks[19], (DEPTH, D_MODEL)),
        "w_ple_gate": nrm(ks[20], (DEPTH, D_MODEL, D_MODEL), D_MODEL ** -0.5),
        "w_ple_proj": nrm(ks[21], (DEPTH, D_PLE, D_MODEL), D_PLE ** -0.5),
        "final_norm": gain(ks[22], (D_MODEL,)),
    }


def reference(x, p, mix_norm, w_in, q_norm, k_norm, conv_w, conv_b, lru_wa, lru_ba,
              lru_wx, lru_bx, lru_lambda, attn_out_norm, lru_out_norm, w_out,
              mlp_norm, w_up, w_down, ple_norm, w_ple_gate, w_ple_proj, final_norm):
    b, s = x.shape[0], x.shape[1]
    cos_r, sin_r, cos_c, sin_c = grid_rope_tables(s)
    h = x
    for l in range(DEPTH):
        hn = rms_norm(h, mix_norm[l])
        z = hn @ w_in[l]
        q, k, v, xr, xg = jnp.split(
            z, [D_ATTN, D_ATTN + D_KV, D_ATTN + 2 * D_KV, D_ATTN + 2 * D_KV + D_LRU], axis=-1)
        q = rms_norm(q.reshape(b, s, N_Q_HEADS, HEAD_DIM), q_norm[l])
        k = rms_norm(k.reshape(b, s, N_KV_HEADS, HEAD_DIM), k_norm[l])
        v = v.reshape(b, s, N_KV_HEADS, HEAD_DIM)
        q = axial_rope(q, cos_r, sin_r, cos_c, sin_c)
        k = axial_rope(k, cos_r, sin_r, cos_c, sin_c)
        q = q.reshape(b, s, N_KV_HEADS, Q_PER_KV, HEAD_DIM)
        y_attn = block_attention(q, k, v)
        xc = lax.conv_general_dilated(
            xr, conv_w[l][:, None, :], window_strides=(1,), padding=[CONV_PAD],
            dimension_numbers=('NWC', 'WIO', 'NWC'), feature_group_count=D_LRU) + conv_b[l]
        hr = bidirectional_rglru(xc, lru_wa[l], lru_ba[l], lru_wx[l], lru_bx[l], lru_lambda[l])
        y_lru = hr * jax.nn.gelu(xg)
        y = jnp.concatenate([rms_norm(y_attn, attn_out_norm[l]), rms_norm(y_lru, lru_out_norm[l])], axis=-1)
        h = h + y @ w_out[l]
        m = rms_norm(h, mlp_norm[l]) @ w_up[l]
        h = h + jnp.square(jax.nn.relu(m)) @ w_down[l]
        gate = jax.nn.sigmoid((rms_norm(h, ple_norm[l]) @ w_ple_gate[l]).astype(jnp.float32)).astype(h.dtype)
        h = h + gate * (p[l] @ w_ple_proj[l])
    return rms_norm(h, final_norm)
```

```python
import numpy as np
import concourse.bass as bass
import concourse.mybir as mybir
from concourse.bass_utils import run_bass_kernel_spmd

F32 = mybir.dt.float32
BF16 = mybir.dt.bfloat16
ALU = mybir.AluOpType
AF = mybir.ActivationFunctionType
AX = mybir.AxisListType

D = 1024
S_FULL = 4096
TOK = 2048
D_IN = 1792
D_FF = 4096
EPS = 1e-6
N_CORES = 8


class Buf:
    __slots__ = ("name", "writers", "readers")

    def __init__(self, name):
        self.name = name
        self.writers = {}
        self.readers = {}


class Op:
    __slots__ = ("idx", "eng", "fn", "deps", "signal", "sigval", "dma_key", "dma_val", "dma_waits", "eidx")

    def __init__(self, idx, eng, fn):
        self.idx = idx
        self.eng = eng
        self.fn = fn
        self.deps = {}
        self.signal = False
        self.sigval = 0
        self.dma_key = None
        self.dma_val = 0
        self.dma_waits = {}
        self.eidx = 0


class Sched:
    ENGS = ("pe", "act", "dve", "pool", "sp")
    WIN = {"pe": 0, "act": 2, "dve": 2, "pool": 8, "sp": 0}

    def __init__(self):
        self.ops = []
        self.dma_count = {}
        self.dma_consumed = {}

    def op(self, eng, fn, reads=(), writes=(), dma_key=None):
        o = Op(len(self.ops), eng, fn)
        me = ("dma", dma_key) if dma_key is not None else eng
        deps = o.deps
        for b in reads:
            for w in b.writers.values():
                deps[w] = True
        for b in writes:
            for w in b.writers.values():
                deps.setdefault(w, False)
            for r in b.readers.values():
                deps.setdefault(r, False)
        for d in list(deps):
            if d.dma_key is not None:
                k = d.dma_key
                o.dma_waits[k] = self.dma_count[k] * 16
                self.dma_consumed[k] = True
                del deps[d]
        if dma_key is not None:
            o.dma_key = dma_key
            c = self.dma_count.get(dma_key, 0)
            if c > 0 and self.dma_consumed.get(dma_key, False):
                o.dma_waits[dma_key] = max(o.dma_waits.get(dma_key, 0), c * 16)
            self.dma_consumed[dma_key] = False
            self.dma_count[dma_key] = c + 1
            o.dma_val = (c + 1) * 16
        for b in reads:
            b.readers[me] = o
        for b in writes:
            b.writers[me] = o
        self.ops.append(o)
        return o

    @staticmethod
    def _needs_sync(o, d, raw):
        if d.eng == o.eng and o.dma_key is None:
            if o.eng == "pe":
                return False
            if not raw and (o.eidx - d.eidx) > Sched.WIN[o.eng]:
                return False
        return True

    def finalize(self):
        cnt_e = {e: 0 for e in self.ENGS}
        for o in self.ops:
            if o.dma_key is None:
                cnt_e[o.eng] += 1
                o.eidx = cnt_e[o.eng]
        for o in self.ops:
            for d, raw in o.deps.items():
                if self._needs_sync(o, d, raw):
                    d.signal = True
        cnt = {e: 0 for e in self.ENGS}
        for o in self.ops:
            if o.dma_key is None and o.signal:
                cnt[o.eng] += 1
                o.sigval = cnt[o.eng]

    def emit(self, nc, block, engsem, dmasem, final_keys):
        streams = {e: [o for o in self.ops if o.eng == e] for e in self.ENGS}

        def run(e, eng):
            waited = {}
            for o in streams[e]:
                need = {}
                for d, raw in o.deps.items():
                    if not self._needs_sync(o, d, raw):
                        continue
                    s = engsem[d.eng]
                    if need.get(s, (0,))[0] < d.sigval:
                        need[s] = (d.sigval, s)
                for k, v in o.dma_waits.items():
                    s = dmasem[k]
                    if need.get(s, (0,))[0] < v:
                        need[s] = (v, s)
                for s, (v, _) in need.items():
                    if waited.get(s, 0) < v:
                        eng.wait_ge(s, v)
                        waited[s] = v
                inst = o.fn(eng)
                if o.dma_key is not None:
                    inst.then_inc(dmasem[o.dma_key], 16)
                elif o.signal:
                    inst.then_inc(engsem[o.eng], 1)
            if e == "sp":
                for k in final_keys:
                    eng.wait_ge(dmasem[k], self.dma_count[k] * 16)

        @block.sync
        def _(eng):
            run("sp", eng)

        @block.gpsimd
        def _(eng):
            run("pool", eng)

        @block.tensor
        def _(eng):
            run("pe", eng)

        @block.vector
        def _(eng):
            run("dve", eng)

        @block.scalar
        def _(eng):
            run("act", eng)


class T:
    __slots__ = ("base", "pstep", "off", "ncols", "buf", "start", "end", "dt")

    def __init__(self, base_ap, off, ncols, buf, start=0, end=0, dt=None):
        self.base = base_ap
        self.pstep = base_ap.ap[0][0]
        self.off = off
        self.ncols = ncols
        self.buf = buf
        self.start = start
        self.end = end
        self.dt = dt

    def v(self, off, dims, p0=0, np_=128):
        return bass.AP(self.base.tensor, p0 * self.pstep + self.off + off,
                       [[self.pstep, np_]] + [list(d) for d in dims])

    def c(self, lo, hi, p0=0, np_=128):
        return self.v(lo, [[1, hi - lo]], p0, np_)

    def full(self):
        return self.v(0, [[1, self.ncols]])


class Arena:
    def __init__(self, ap_f32, ap_bf16, nbytes):
        self.views = {F32: ap_f32, BF16: ap_bf16}
        self.free = [(0, nbytes)]
        self.dead = []
        self.peak = 0

    def alloc(self, name, ncols, dt):
        esz = 4 if dt == F32 else 2
        nbytes = (ncols * esz + 63) // 64 * 64
        for i, (s, e) in enumerate(self.free):
            if e - s >= nbytes:
                start = s
                if e - s == nbytes:
                    self.free.pop(i)
                else:
                    self.free[i] = (s + nbytes, e)
                break
        else:
            raise RuntimeError(f"arena OOM allocating {name} ({nbytes} B); free={self.free}")
        end = start + nbytes
        self.peak = max(self.peak, end)
        b = Buf(name)
        for (ds, de, db) in self.dead:
            if ds < end and de > start:
                for k, o in db.writers.items():
                    if k not in b.writers or b.writers[k].idx < o.idx:
                        b.writers[k] = o
                for k, o in db.readers.items():
                    if k not in b.readers or b.readers[k].idx < o.idx:
                        b.readers[k] = o
        return T(self.views[dt], start // esz, ncols, b, start, end, dt)

    def release(self, *tiles):
        for t in tiles:
            self.dead.append((t.start, t.end, t.buf))
            self.free.append((t.start, t.end))
        self.free.sort()
        merged = []
        for s, e in self.free:
            if merged and merged[-1][1] == s:
                merged[-1] = (merged[-1][0], e)
            else:
                merged.append((s, e))
        self.free = merged


STAGE = {"A": 0, "C": 1, "B": 2, "G": 3}


def build_program(stop_after="G", dumps=()):
    nc = bass.Bass("TRN2", target_bir_lowering=False)
    stop_n = STAGE[stop_after]
    S = Sched()

    def din(name, shape):
        return nc.dram_tensor(name, list(shape), F32, kind="ExternalInput").ap()

    x_own = din("x_own", [TOK, D])
    x_oth = din("x_oth", [TOK, D])
    p_own = din("p_own", [TOK, 256])
    w_in = din("w_in", [D, D_IN])
    ropeC = din("ropeC", [S_FULL, 64])
    ropeS = din("ropeS", [S_FULL, 64])
    gqk = din("gqk", [1, 256])
    prm_d = din("prm", [128, 96])
    wbd_d = din("wbd", [16 * 128, 128])
    w_out = din("w_out", [D, D])
    w_up = din("w_up", [D, D_FF])
    w_down = din("w_down", [D_FF, D])
    w_gate = din("w_gate", [D, D])
    w_proj = din("w_proj", [256, D])
    fing = din("fing", [1, D])
    out_d = nc.dram_tensor("out", [TOK, D], F32, kind="ExternalOutput").ap()
    dump_d = {}
    for (nm, ncols) in dumps:
        dump_d[nm] = nc.dram_tensor("dbg_" + nm, [128, ncols], F32, kind="ExternalOutput").ap()

    AW = 52736
    from contextlib import ExitStack
    with ExitStack() as es:
        arena_t = es.enter_context(nc.sbuf_tensor("arena", [128, AW], F32))
        ps_t = es.enter_context(nc.psum_tensor("ps", [128, 4096], F32))
        a_f32 = arena_t[:, :]
        a_bf = a_f32.bitcast(BF16)
        AR = Arena(a_f32, a_bf, AW * 4)
        ps_f32 = ps_t[:, :]
        ps_bf = ps_f32.bitcast(BF16)
        PB = [Buf(f"psum{i}") for i in range(8)]

        def PS(bank, off, dims, p0=0, np_=128):
            pstep = ps_f32.ap[0][0]
            return bass.AP(ps_f32.tensor, p0 * pstep + bank * 512 + off,
                           [[pstep, np_]] + [list(d) for d in dims])

        def PSB(bank, off, dims, p0=0, np_=128):
            pstep = ps_bf.ap[0][0]
            return bass.AP(ps_bf.tensor, p0 * pstep + bank * 1024 + off,
                           [[pstep, np_]] + [list(d) for d in dims])

        def bufs(xs):
            out = []
            for x in xs:
                out.append(x.buf if isinstance(x, T) else x)
            return out

        def mm(out, lhsT, rhs, start, stop, rd, wr):
            S.op("pe", lambda e: e.matmul(out, lhsT=lhsT, rhs=rhs, start=start, stop=stop),
                 bufs(rd), bufs(wr))

        def tr(out, in_, ident, rd, wr):
            S.op("pe", lambda e: e.transpose(out=out, in_=in_, identity=ident), bufs(rd), bufs(wr))

        def act(out, in_, func, rd, wr, bias=None, scale=None, accum=None):
            kw = {}
            if bias is not None:
                kw["bias"] = bias
            if scale is not None:
                kw["scale"] = scale
            if accum is not None:
                kw["accum_out"] = accum
            S.op("act", lambda e: e.activation(out=out, in_=in_, func=func, **kw), bufs(rd), bufs(wr))

        def tt(eng, out, in0, in1, op, rd, wr):
            S.op(eng, lambda e: e.tensor_tensor(out=out, in0=in0, in1=in1, op=op), bufs(rd), bufs(wr))

        def ts(eng, out, in0, s1, op0, rd, wr, s2=None, op1=None):
            if op1 is None:
                S.op(eng, lambda e: e.tensor_scalar(out=out, in0=in0, scalar1=s1, scalar2=None, op0=op0),
                     bufs(rd), bufs(wr))
            else:
                S.op(eng, lambda e: e.tensor_scalar(out=out, in0=in0, scalar1=s1, scalar2=s2, op0=op0, op1=op1),
                     bufs(rd), bufs(wr))

        def stt(out, in0, scalar, in1, op0, op1, rd, wr):
            S.op("dve", lambda e: e.scalar_tensor_tensor(out=out, in0=in0, scalar=scalar, in1=in1, op0=op0, op1=op1),
                 bufs(rd), bufs(wr))

        def cp(eng, out, in_, rd, wr):
            S.op(eng, lambda e: e.tensor_copy(out=out, in_=in_), bufs(rd), bufs(wr))

        def recip(out, in_, rd, wr):
            S.op("dve", lambda e: e.reciprocal(out=out, in_=in_), bufs(rd), bufs(wr))

        def reduce_(out, in_, axis, op, rd, wr, absv=None):
            if absv:
                S.op("dve", lambda e: e.tensor_reduce(out=out, in_=in_, axis=axis, op=op, apply_absolute_value=True),
                     bufs(rd), bufs(wr))
            else:
                S.op("dve", lambda e: e.tensor_reduce(out=out, in_=in_, axis=axis, op=op), bufs(rd), bufs(wr))

        def memset(eng, ap, val, wr):
            S.op(eng, lambda e: e.memset(ap, val), [], bufs(wr))

        def dma(q, out, in_, key, rd, wr):
            S.op(q, lambda e: e.dma_start(out=out, in_=in_), bufs(rd), bufs(wr), dma_key=key)

        def scan(out, d0, d1, init, rd, wr):
            S.op("dve", lambda e: e.tensor_tensor_scan(out=out, data0=d0, data1=d1, initial=init,
                                                       op0=ALU.mult, op1=ALU.add), bufs(rd), bufs(wr))

        identf = AR.alloc("identf", 128, F32)
        identb = AR.alloc("identb", 128, BF16)
        onesf = AR.alloc("onesf", 64, F32)
        onesb = AR.alloc("onesb", 2, BF16)
        prm = AR.alloc("prm", 96, F32)
        gq = AR.alloc("gq", 256, F32)
        misc = AR.alloc("misc", 96, F32)
        P_MIX, P_MLP, P_PLE, P_CONV, P_CONVB, P_LBA, P_LBX, P_LAM, P_AG, P_LG = 0, 8, 16, 24, 44, 48, 56, 64, 72, 80
        M_CL, M_CL2, M_NB, M_TMP, M_THI, M_TLO, M_HBA, M_HBX, M_HCL = 0, 8, 16, 17, 24, 44, 64, 72, 80

        memset("pool", identf.full(), 0.0, [identf])
        memset("pool", onesf.full(), 1.0, [onesf])
        memset("pool", onesb.full(), 1.0, [onesb])
        mhalf = AR.alloc("mhalf", 16, F32)
        memset("pool", mhalf.full(), -0.5, [mhalf])
        S.op("pool", lambda e: e.affine_select(out=identf.full(), in_=identf.full(), pattern=[[-1, 128]],
                                               compare_op=ALU.not_equal, fill=1.0, base=0, channel_multiplier=1),
             [identf.buf], [identf.buf])
        cp("pool", identb.full(), identf.full(), [identf], [identb])
        dma("sp", prm.full(), prm_d[:, :], "prm", [], [prm])
        dma("sp", gq.full(), gqk.partition_broadcast(128), "prm", [], [gq])

        act(misc.c(M_TMP, M_TMP + 8), prm.c(P_LAM, P_LAM + 8), AF.Exp, [prm], [misc], scale=-1.0)
        act(misc.c(M_TMP, M_TMP + 8), misc.c(M_TMP, M_TMP + 8), AF.Ln, [misc], [misc], bias=1.0)
        ts("dve", misc.c(M_CL, M_CL + 8), misc.c(M_TMP, M_TMP + 8), -8.0, ALU.mult, [misc], [misc])
        ts("dve", misc.c(M_CL2, M_CL2 + 8), misc.c(M_TMP, M_TMP + 8), -16.0, ALU.mult, [misc], [misc])
        ts("dve", misc.c(M_HCL, M_HCL + 8), misc.c(M_TMP, M_TMP + 8), -4.0, ALU.mult, [misc], [misc])
        ts("dve", misc.c(M_HBA, M_HBA + 8), prm.c(P_LBA, P_LBA + 8), 0.5, ALU.mult, [prm], [misc])
        ts("dve", misc.c(M_HBX, M_HBX + 8), prm.c(P_LBX, P_LBX + 8), 0.5, ALU.mult, [prm], [misc])
        reduce_(misc.c(M_TMP, M_TMP + 1), gq.c(0, 64), AX.X, ALU.max, [gq], [misc], absv=True)
        reduce_(misc.c(M_TMP + 1, M_TMP + 2), gq.c(128, 192), AX.X, ALU.max, [gq], [misc], absv=True)
        tt("dve", misc.c(M_NB, M_NB + 1), misc.c(M_TMP, M_TMP + 1), misc.c(M_TMP + 1, M_TMP + 2), ALU.mult, [misc], [misc])
        ts("dve", misc.c(M_NB, M_NB + 1), misc.c(M_NB, M_NB + 1), -8.0, ALU.mult, [misc], [misc])
        thb = AR.alloc("thb", 20, BF16)
        cp("dve", thb.full(), prm.c(P_CONV, P_CONV + 20), [prm], [thb])
        cp("dve", misc.c(M_THI, M_THI + 20), thb.full(), [thb], [misc])
        tt("dve", misc.c(M_TLO, M_TLO + 20), prm.c(P_CONV, P_CONV + 20), misc.c(M_THI, M_THI + 20), ALU.subtract,
           [prm, misc], [misc])
        QT = [AR.alloc(f"QT{i}", 4 * 512, BF16) for i in range(4)]
        KT = [AR.alloc(f"KT{i}", 512, BF16) for i in range(8)]
        VX = [AR.alloc(f"VX{i}", 4 * 256, BF16) for i in range(8)]
        XR = [AR.alloc(f"XR{i}", 4100, BF16) for i in range(4)]
        GX = [AR.alloc(f"GX{i}", TOK, BF16) for i in range(4)]
        for i in range(8):
            memset("pool", VX[i].v(64, [[256, 4], [1, 128]]), 1.0, [VX[i]])
        for i in range(4):
            memset("pool", XR[i].c(0, 2), 0.0, [XR[i]])
            memset("pool", XR[i].c(4098, 4100), 0.0, [XR[i]])

        winq = AR.alloc("winq", 8 * 768, BF16)
        winf = AR.alloc("winf", 8 * 1024, BF16)
        w_in_v = w_in.rearrange("(c p) n -> p c n", p=128)
        for c in range(8):
            dma("pool", winf.v(c * 1024, [[1, 1024]]), w_in_v[:, c, 768:D_IN], "winf", [], [winf])
        xt = [AR.alloc(f"xt{i}", D, F32) for i in range(5)]
        xs = [AR.alloc(f"xs{i}", D, BF16) for i in range(2)]
        junk = [AR.alloc(f"junk{i}", D, BF16) for i in range(2)]
        jn = [0]
        hnT = [AR.alloc(f"hnT{i}", 8 * 512, BF16) for i in range(3)]
        ssqA = [AR.alloc(f"ssqA{i}", 8, F32) for i in range(2)]
        tabC = [AR.alloc(f"tabC{i}", 256, F32) for i in range(2)]
        tabS = [AR.alloc(f"tabS{i}", 256, F32) for i in range(2)]
        GCq = [AR.alloc(f"GCq{i}", 256, F32) for i in range(2)]
        GSq = [AR.alloc(f"GSq{i}", 256, F32) for i in range(2)]
        GCk = [AR.alloc(f"GCk{i}", 256, F32) for i in range(2)]
        GSk = [AR.alloc(f"GSk{i}", 256, F32) for i in range(2)]
        SCR = [[AR.alloc(f"sq{k}", 640, F32), AR.alloc(f"qn{k}", 640, F32), AR.alloc(f"t2{k}", 640, F32),
                AR.alloc(f"qr{k}", 640, BF16), AR.alloc(f"sst{k}", 32, F32)] for k in range(2)]

        trb = [6, 7]
        trn = [0]

        def next_trb():
            b = trb[trn[0] % 2]
            trn[0] += 1
            return b

        def stage1_tabs(blk):
            own = blk < 4
            sl = blk % 2
            gl = blk * 512
            dma("sp", tabC[sl].v(0, [[64, 4], [1, 64]]), ropeC[gl:gl + 512, :].rearrange("(t p) f -> p t f", p=128),
                f"tab{sl}", [], [tabC[sl]])
            dma("sp", tabS[sl].v(0, [[64, 4], [1, 64]]), ropeS[gl:gl + 512, :].rearrange("(t p) f -> p t f", p=128),
                f"tab{sl}", [], [tabS[sl]])
            if own:
                tt("dve", GCq[sl].v(0, [[64, 4], [1, 64]]), tabC[sl].v(0, [[64, 4], [1, 64]]),
                   gq.v(0, [[0, 4], [1, 64]]), ALU.mult, [tabC[sl], gq], [GCq[sl]])
                tt("dve", GSq[sl].v(0, [[64, 4], [1, 64]]), tabS[sl].v(0, [[64, 4], [1, 64]]),
                   gq.v(64, [[0, 4], [1, 64]]), ALU.mult, [tabS[sl], gq], [GSq[sl]])
            tt("dve", GCk[sl].v(0, [[64, 4], [1, 64]]), tabC[sl].v(0, [[64, 4], [1, 64]]),
               gq.v(128, [[0, 4], [1, 64]]), ALU.mult, [tabC[sl], gq], [GCk[sl]])
            tt("dve", GSk[sl].v(0, [[64, 4], [1, 64]]), tabS[sl].v(0, [[64, 4], [1, 64]]),
               gq.v(192, [[0, 4], [1, 64]]), ALU.mult, [tabS[sl], gq], [GSk[sl]])

        def stage1_load(blk, i):
            own = blk < 4
            src = x_own if own else x_oth
            r0 = (blk % 4) * 512
            tile = blk * 4 + i
            x_t = xt[tile % 5]
            dma("sp", x_t.full(), src[r0 + i * 128:r0 + (i + 1) * 128, :], f"xt{tile % 5}", [], [x_t])

        def stage1_pre(blk, i):
            sl = blk % 2
            tile = blk * 4 + i
            x_t = xt[tile % 5]
            jk = junk[jn[0] % 2]
            jn[0] += 1
            sa = ssqA[tile % 2]
            act(jk.full(), x_t.full(), AF.Square, [x_t], [jk, sa], accum=sa.c(0, 1))
            act(sa.c(1, 2), sa.c(0, 1), AF.Sqrt, [sa], [sa], scale=1.0 / D, bias=EPS)
            recip(sa.c(2, 3), sa.c(1, 2), [sa], [sa])
            xs_t = xs[tile % 2]
            ts("dve", xs_t.full(), x_t.full(), sa.c(2, 3), ALU.mult, [x_t, sa], [xs_t])

        def stage1_post(blk, i):
            sl = blk % 2
            tile = blk * 4 + i
            xs_t = xs[tile % 2]
            b = next_trb()
            for c in range(8):
                tr(PSB(b, c * 128, [[1, 128]]), xs_t.c(c * 128, (c + 1) * 128), identb.full(),
                   [xs_t, identb], [PB[b]])
            tt("dve", hnT[blk % 3].v(i * 128, [[512, 8], [1, 128]]), PSB(b, 0, [[128, 8], [1, 128]]),
               prm.v(P_MIX, [[1, 8], [0, 128]]), ALU.mult, [PB[b], prm], [hnT[blk % 3]])

        qkv_pairs = [(0, 1), (2, 3)]
        qkvn = [0]
        fmb = [4, 5]
        fmn = [0]
        pending = []

        def post_part1(blk, i, b0, own, k):
            sl = blk % 2
            sq, qn, t2, qr, sst = SCR[k]
            t1 = sq
            R = [PB[b0], PB[b0 + 1]] if own else [PB[b0]]
            ko = 512 if own else 0
            vo = ko + 128
            W = 640 if own else 128
            so = 0 if own else 512
            act(sq.c(so, 640), PS(b0, 0, [[1, W]]), AF.Square, R, [sq])
            act(VX[blk].v(i * 256, [[192, 2], [1, 64]]), PS(b0, vo, [[64, 2], [1, 64]]), AF.Copy, R, [VX[blk]])
            if own:
                reduce_(sst.c(0, 8), sq.v(0, [[32, 8], [256, 2], [1, 32]]), AX.XY, ALU.add, [sq], [sst])
            reduce_(sst.c(8, 10), sq.v(512, [[32, 2], [64, 2], [1, 32]]), AX.XY, ALU.add, [sq], [sst])
            lo = 0 if own else 8
            act(sst.c(10 + lo, 20), sst.c(lo, 10), AF.Sqrt, [sst], [sst], scale=1.0 / 64, bias=EPS)
            recip(sst.c(20 + lo, 30), sst.c(10 + lo, 20), [sst], [sst])
            if own:
                tt("dve", qn.v(0, [[256, 2], [32, 8], [1, 32]]), PS(b0, 0, [[256, 2], [32, 8], [1, 32]]),
                   sst.v(20, [[0, 2], [1, 8], [0, 32]]), ALU.mult, R + [sst], [qn])
            tt("dve", qn.v(512, [[64, 2], [32, 2], [1, 32]]), PS(b0, ko, [[64, 2], [32, 2], [1, 32]]),
               sst.v(28, [[0, 2], [1, 2], [0, 32]]), ALU.mult, R + [sst], [qn])
            if own:
                tt("dve", t1.v(0, [[256, 2], [32, 8], [1, 32]]), qn.v(0, [[256, 2], [32, 8], [1, 32]]),
                   GCq[sl].v(i * 64, [[32, 2], [0, 8], [1, 32]]), ALU.mult, [qn, GCq[sl]], [t1])
                tt("dve", t2.v(0, [[256, 2], [32, 8], [1, 32]]), qn.v(256, [[-256, 2], [32, 8], [1, 32]]),
                   GSq[sl].v(i * 64, [[32, 2], [0, 8], [1, 32]]), ALU.mult, [qn, GSq[sl]], [t2])
            tt("dve", t1.v(512, [[64, 2], [32, 2], [1, 32]]), qn.v(512, [[64, 2], [32, 2], [1, 32]]),
               GCk[sl].v(i * 64, [[32, 2], [0, 2], [1, 32]]), ALU.mult, [qn, GCk[sl]], [t1])
            tt("dve", t2.v(512, [[64, 2], [32, 2], [1, 32]]), qn.v(576, [[-64, 2], [32, 2], [1, 32]]),
               GSk[sl].v(i * 64, [[32, 2], [0, 2], [1, 32]]), ALU.mult, [qn, GSk[sl]], [t2])
            if own:
                tt("dve", qr.v(0, [[32, 2], [64, 8], [1, 32]]), t1.v(0, [[256, 2], [32, 8], [1, 32]]),
                   t2.v(0, [[256, 2], [32, 8], [1, 32]]), ALU.add, [t1, t2], [qr])
            tt("dve", qr.v(512, [[32, 2], [64, 2], [1, 32]]), t1.v(512, [[64, 2], [32, 2], [1, 32]]),
               t2.v(512, [[64, 2], [32, 2], [1, 32]]), ALU.add, [t1, t2], [qr])

        def post_part2(blk, i, own, k):
            qr = SCR[k][3]
            b = next_trb()
            if own:
                for g in range(4):
                    tr(PSB(b, g * 128, [[1, 128]]), qr.c(g * 128, (g + 1) * 128), identb.full(), [qr, identb], [PB[b]])
            tr(PSB(b, 512, [[1, 128]]), qr.c(512, 640), identb.full(), [qr, identb], [PB[b]])
            if own:
                act(QT[blk].v(i * 128, [[512, 4], [1, 128]]), PSB(b, 0, [[128, 4], [1, 128]]), AF.Copy,
                    [PB[b]], [QT[blk]])
            act(KT[blk].c(i * 128, (i + 1) * 128), PSB(b, 512, [[1, 128]]), AF.Copy, [PB[b]], [KT[blk]])

        def fm_group(blk, j):
            h_ = hnT[blk % 3]
            b = fmb[fmn[0] % 2]
            fmn[0] += 1
            for c in range(8):
                mm(PS(b, 0, [[1, 512]]), winf.c(c * 1024 + j * 128, c * 1024 + (j + 1) * 128),
                   h_.c(c * 512, (c + 1) * 512), c == 0, c == 7, [h_, winf], [PB[b]])
            if j < 4:
                act(XR[j].c(2 + blk * 512, 2 + (blk + 1) * 512), PS(b, 0, [[1, 512]]), AF.Copy, [PB[b]], [XR[j]])
            else:
                act(GX[j - 4].c(blk * 512, (blk + 1) * 512), PS(b, 0, [[1, 512]]), AF.Copy, [PB[b]], [GX[j - 4]])

        def stage2_tile(blk, i, nxt):
            if nxt is not None:
                stage1_pre(*nxt)
            own = blk < 4
            sl = blk % 2
            h_ = hnT[blk % 3]
            tile = blk * 4 + i
            k = tile % 2
            b0, b1 = qkv_pairs[qkvn[0] % 2]
            qkvn[0] += 1
            if own:
                for c in range(8):
                    mm(PS(b0, 0, [[1, 512]]), h_.c(c * 512 + i * 128, c * 512 + (i + 1) * 128),
                       winq.c(c * 768, c * 768 + 512), c == 0, c == 7, [h_, winq], [PB[b0]])
                for c in range(8):
                    mm(PS(b1, 0, [[1, 256]]), h_.c(c * 512 + i * 128, c * 512 + (i + 1) * 128),
                       winq.c(c * 768 + 512, c * 768 + 768), c == 0, c == 7, [h_, winq], [PB[b1]])
            else:
                for c in range(8):
                    mm(PS(b0, 0, [[1, 256]]), h_.c(c * 512 + i * 128, c * 512 + (i + 1) * 128),
                       winq.c(c * 768 + 512, c * 768 + 768), c == 0, c == 7, [h_, winq], [PB[b0]])
            post_part1(blk, i, b0, own, k)
            if own:
                fm_group(blk, 2 * i)
                fm_group(blk, 2 * i + 1)
            else:
                fm_group(blk, i)
            if nxt is not None:
                stage1_post(*nxt)
            if pending:
                post_part2(*pending.pop())
            pending.append((blk, i, own, k))

        def s1(f_, s_):
            f_(s_ // 4, s_ % 4)

        stage1_tabs(0)
        for s_ in range(5):
            s1(stage1_load, s_)
        wq_stg = []
        for c in range(8):
            xr_t = XR[c // 2]
            stg = T(AR.views[F32], xr_t.start // 4 + 1 + (c % 2) * 768, 768, xr_t.buf)
            wq_stg.append((stg, xr_t))
            dma("sp", stg.full(), w_in_v[:, c, 0:768], f"wq{c}", [], [xr_t])
        for s_ in range(4):
            s1(stage1_pre, s_)
            s1(stage1_post, s_)
            if s_ == 0:
                s1(stage1_load, 5)
        for c in range(8):
            stg, xr_t = wq_stg[c]
            if c % 2 == 0:
                act(winq.c(c * 768, (c + 1) * 768), stg.full(), AF.Copy, [xr_t], [winq])
            else:
                cp("dve", winq.c(c * 768, (c + 1) * 768), stg.full(), [xr_t], [winq])
        s1(stage1_pre, 4)
        s1(stage1_post, 4)
        for blk in range(8):
            if blk + 1 < 8:
                stage1_tabs(blk + 1)
            for i in range(4):
                s_ = 4 * blk + i + 5
                if s_ + 1 < 32:
                    s1(stage1_load, s_ + 1)
                stage2_tile(blk, i, (s_ // 4, s_ % 4) if s_ < 32 else None)
        post_part2(*pending.pop())
        AR.release(winq, winf, *xt, *xs, *junk, *hnT, *ssqA, *tabC, *tabS, *GCq, *GSq, *GCk, *GSk)
        for k in range(2):
            AR.release(*SCR[k])

        dump_src = {}
        if stop_after == "A":
            dump_src.update({"QT0": QT[0], "KT0": KT[0], "KT5": KT[5], "VX0": VX[0], "XR0": XR[0], "GX0": GX[0],
                             "QT3": QT[3]})


        prs = [(0, 1), (2, 3), (4, 5), (6, 7)]
        prn = [0]

        def next_pair():
            p_ = prs[prn[0] % 4]
            prn[0] += 1
            return p_

        def rev(t_, off, n):
            return t_.v(off + n - 1, [[-1, n]])

        if stop_n >= STAGE["C"]:
            wbd = AR.alloc("wbd", 16 * 128, BF16)
            dma("pool", wbd.v(0, [[128, 16], [1, 128]]), wbd_d.rearrange("(k p) m -> p k m", p=128), "wbd", [], [wbd])
            dg = AR.alloc("dg", 40 * 128, BF16)

            def build_dg(k0, k1):
                for k in range(k0, k1):
                    ts("dve", dg.c((2 * k) * 128, (2 * k + 1) * 128), identf.full(), misc.c(M_THI + k, M_THI + k + 1),
                       ALU.mult, [identf, misc], [dg])
                    ts("dve", dg.c((2 * k + 1) * 128, (2 * k + 2) * 128), identf.full(), misc.c(M_TLO + k, M_TLO + k + 1),
                       ALU.mult, [identf, misc], [dg])

            build_dg(0, 5)
            YL = [AR.alloc(f"YL{i}", TOK, BF16) for i in range(4)]
            YLQ = AR.alloc("YLQ", TOK, BF16)
            ssq_l = AR.alloc("ssq_l", 32, F32)
            xc = [AR.alloc(f"xc{i}", 1024, F32) for i in range(2)]
            xcb = [AR.alloc(f"xcb{i}", 1024, BF16) for i in range(2)]
            sets = [[AR.alloc(f"lr{i}", 1024, F32), AR.alloc(f"la{i}", 1024, F32), AR.alloc(f"li{i}", 1024, F32)]
                    for i in range(4)]
            setn = [0]
            HF = [AR.alloc(f"HF{i}", 1024, F32) for i in range(2)]
            HB = AR.alloc("HB", 2048, F32)

            def conv(cc, tb, slot):
                pr = next_pair()
                for half in range(2):
                    n = 0
                    for j in range(5):
                        for hl in range(2):
                            k = ((cc * 5 + j) * 2 + hl) * 128
                            o = tb * 1024 + half * 512 + j
                            mm(PS(pr[half], 0, [[1, 512]]), dg.c(k, k + 128), XR[cc].c(o, o + 512),
                               n == 0, n == 9, [dg, XR[cc]], [PB[pr[half]]])
                            n += 1
                act(xc[slot].full(), PS(pr[0], 0, [[1, 1024]]), AF.Identity, [PB[pr[0]], PB[pr[1]], prm], [xc[slot]],
                    bias=prm.c(P_CONVB + cc, P_CONVB + cc + 1))
                cp("dve", xcb[slot].full(), xc[slot].full(), [xc[slot]], [xcb[slot]])

            def gate_pre(cc, e, slot):
                pa = next_pair()
                px = next_pair()
                for ty, pp in ((0, pa), (1, px)):
                    k = ((e * 2 + ty) * 4 + cc) * 128
                    for half in range(2):
                        mm(PS(pp[half], 0, [[1, 512]]), wbd.c(k, k + 128), xcb[slot].c(half * 512, half * 512 + 512),
                           True, True, [wbd, xcb[slot]], [PB[pp[half]]])
                st_ = sets[setn[0] % 4]
                setn[0] += 1
                Rr, Aa, Ii = st_
                col = e * 4 + cc
                act(Rr.full(), PS(pa[0], 0, [[1, 1024]]), AF.Tanh, [PB[pa[0]], PB[pa[1]], misc], [Rr],
                    bias=misc.c(M_HBA + col, M_HBA + col + 1), scale=0.5)
                act(Ii.full(), PS(px[0], 0, [[1, 1024]]), AF.Tanh, [PB[px[0]], PB[px[1]], misc], [Ii],
                    bias=misc.c(M_HBX + col, M_HBX + col + 1), scale=0.5)
                act(Aa.full(), Rr.full(), AF.Exp, [Rr, misc], [Aa], scale=misc.c(M_HCL + col, M_HCL + col + 1),
                    bias=misc.c(M_HCL + col, M_HCL + col + 1))
                stt(Ii.full(), Ii.full(), 1.0, xc[slot].full(), ALU.add, ALU.mult, [Ii, xc[slot]], [Ii])
                tt("dve", Rr.full(), Aa.full(), Aa.full(), ALU.mult, [Aa], [Rr])
                return st_

            def gate_sqrt(st_):
                Rr = st_[0]
                act(Rr.full(), Rr.full(), AF.Sqrt, [Rr], [Rr], scale=-1.0, bias=1.0)

            def gate_scan(e, st_, hout, ho, init, init_rd):
                Rr, Aa, Ii = st_
                stt(Ii.full(), Ii.full(), 0.5, Rr.full(), ALU.mult, ALU.mult, [Ii, Rr], [Ii])
                if e == 0:
                    scan(hout.c(ho, ho + 1024), Aa.full(), Ii.full(), init, [Aa, Ii] + init_rd, [hout])
                else:
                    scan(rev(hout, ho, 1024), rev(Aa, 0, 1024), rev(Ii, 0, 1024), init, [Aa, Ii] + init_rd, [hout])

            def combine(cc, on_act=True):
                for tbo in range(2):
                    tt("dve", HF[tbo].full(), HF[tbo].full(), HB.c(tbo * 1024, tbo * 1024 + 1024), ALU.add,
                       [HF[tbo], HB], [HF[tbo]])
                    tt("dve", HF[tbo].full(), HF[tbo].full(), GX[cc].c(tbo * 1024, tbo * 1024 + 1024), ALU.mult,
                       [HF[tbo], GX[cc]], [HF[tbo]])
                    if on_act:
                        act(YLQ.c(tbo * 1024, tbo * 1024 + 1024), HF[tbo].full(), AF.Square, [HF[tbo]], [YLQ])
                        act(YL[cc].c(tbo * 1024, tbo * 1024 + 1024), HF[tbo].full(), AF.Copy, [HF[tbo], prm], [YL[cc]],
                            scale=prm.c(P_LG + cc, P_LG + cc + 1))
                    else:
                        tt("dve", YLQ.c(tbo * 1024, tbo * 1024 + 1024), HF[tbo].full(), HF[tbo].full(), ALU.mult,
                           [HF[tbo]], [YLQ])
                        ts("dve", YL[cc].c(tbo * 1024, tbo * 1024 + 1024), HF[tbo].full(),
                           prm.c(P_LG + cc, P_LG + cc + 1), ALU.mult, [HF[tbo], prm], [YL[cc]])

            def combine_pe(cc, pr=None):
                if pr is None:
                    pr = next_pair()
                for i in range(16):
                    mm(PS(pr[0], i, [[1, 1]]), YLQ.c(i * 128, (i + 1) * 128), onesb.c(0, 1), True, True,
                       [YLQ, onesb], [PB[pr[0]]])
                if cc == 0:
                    cp("dve", ssq_l.c(0, 16), PS(pr[0], 0, [[1, 16]]), [PB[pr[0]]], [ssq_l])
                else:
                    tt("dve", ssq_l.c(0, 16), ssq_l.c(0, 16), PS(pr[0], 0, [[1, 16]]), ALU.add, [ssq_l, PB[pr[0]]], [ssq_l])

            conv(0, 3, 0)
            conv(0, 2, 1)
            build_dg(5, 20)
            for cc_ in range(4):
                act(GX[cc_].full(), GX[cc_].full(), AF.Gelu, [GX[cc_]], [GX[cc_]])
            for cc in range(4):
                s3 = gate_pre(cc, 1, 0)
                s2 = gate_pre(cc, 1, 1)
                conv(cc, 1, 0)
                conv(cc, 0, 1)
                if cc >= 1:
                    combine(cc - 1)
                gate_sqrt(s3)
                gate_sqrt(s2)
                gate_scan(1, s3, HF[1], 0, 0.0, [])
                gate_scan(1, s2, HF[0], 0, HF[1].c(0, 1), [HF[1]])
                s1 = gate_pre(cc, 1, 0)
                s0 = gate_pre(cc, 1, 1)
                if cc >= 1:
                    combine_pe(cc - 1)
                gate_sqrt(s1)
                gate_sqrt(s0)
                gate_scan(1, s1, HB, 1024, HF[0].c(0, 1), [HF[0]])
                gate_scan(1, s0, HB, 0, HB.c(1024, 1025), [HB])
                f0 = gate_pre(cc, 0, 1)
                f1 = gate_pre(cc, 0, 0)
                if cc + 1 < 4:
                    conv(cc + 1, 3, 0)
                    conv(cc + 1, 2, 1)
                gate_sqrt(f0)
                gate_sqrt(f1)
                gate_scan(0, f0, HF[0], 0, 0.0, [])
                gate_scan(0, f1, HF[1], 0, HF[0].c(1023, 1024), [HF[0]])
            defer_c3 = stop_n >= STAGE["G"]
            if defer_c3:
                AR.release(*xc, *xcb, *XR, GX[0], GX[1], GX[2], dg, wbd, thb)
            else:
                combine(3)
                combine_pe(3)
                AR.release(YLQ, *xc, *xcb, *HF, HB, *XR, *GX, dg, wbd, thb)
            for st_ in sets:
                AR.release(*st_)
            if stop_n == STAGE["C"]:
                dump_src.update({"YL0": YL[0], "YL3": YL[3], "ssq_l": ssq_l})

        if stop_n >= STAGE["B"]:
            YAP = [AR.alloc(f"YAP{i}", TOK, BF16) for i in range(4)]
            ssq_a = AR.alloc("ssq_a", 32, F32)
            PT = [AR.alloc(f"PT{i}", 1024, BF16) for i in range(3)]
            rd = [AR.alloc(f"rd{i}", 512, F32) for i in range(2)]
            bcs = [AR.alloc(f"bcs{i}", 512, F32) for i in range(2)]
            y0 = [AR.alloc(f"y0{i}", 512, F32) for i in range(2)]
            ysq = AR.alloc("ysq", 4 * 512, BF16)
            if stop_n >= STAGE["G"]:
                WoA = AR.alloc("WoA", 4 * 1024, BF16)
                WoL = AR.alloc("WoL", 4 * 1024, BF16)
                dma("pool", WoA.v(0, [[1024, 4], [1, 1024]]),
                    w_out[0:512, :].rearrange("(g q) n -> q g n", q=128), "wo", [], [WoA])
                dma("pool", WoL.v(0, [[1024, 4], [1, 1024]]),
                    w_out[512:1024, :].rearrange("(c p) n -> p c n", p=128), "wo", [], [WoL])
                H = [AR.alloc(f"H{i}", D, F32) for i in range(16)]
                for i in range(16):
                    dma("sp", H[i].full(), x_own[i * 128:(i + 1) * 128, :], f"hx{i // 4}", [], [H[i]])
            sprs = [(0, 1), (2, 3)]
            uprs = [(4, 5), (6, 7)]
            ptn = [0]
            scale = 64 ** -0.5

            def qk(n, sp):
                qb, g, kt = n // 128, (n // 32) % 4, n % 32
                kT = KT[kt // 4]
                ko = (kt % 4) * 128
                for h2 in range(2):
                    mm(PS(sp[h2], 0, [[1, 512]]), kT.c(ko, ko + 128, p0=64 * h2, np_=64),
                       QT[qb].c(g * 512, (g + 1) * 512, p0=64 * h2, np_=64), True, True, [kT, QT[qb]], [PB[sp[h2]]])

            def epi1(up):
                for h2 in range(2):
                    dr = 64 if h2 == 0 else 0
                    p0 = 64 * h2
                    recip(rd[h2].c(0, 512, p0=dr, np_=1), PS(up[h2], 0, [[1, 512]], p0=dr, np_=1), [PB[up[h2]]], [rd[h2]])
                    src = bass.AP(rd[h2].base.tensor, dr * rd[h2].pstep + rd[h2].off, [[rd[h2].pstep, 1], [0, 64], [1, 512]])
                    dma("sp", bcs[h2].c(0, 512, p0=p0, np_=64), src, f"bc{h2}", [rd[h2]], [bcs[h2]])

            def epi2(qb, g, up, bp):
                for h2 in range(2):
                    p0 = 64 * h2
                    tt("dve", y0[h2].c(0, 512, p0=p0, np_=64), PS(up[h2], 0, [[1, 512]], p0=p0, np_=64),
                       bcs[h2].c(0, 512, p0=p0, np_=64), ALU.mult, [PB[up[h2]], bcs[h2]], [y0[h2]])
                    tt("dve", ysq.c(g * 512, (g + 1) * 512, p0=p0, np_=64), y0[h2].c(0, 512, p0=p0, np_=64),
                       y0[h2].c(0, 512, p0=p0, np_=64), ALU.mult, [y0[h2]], [ysq])
                    ts("dve", YAP[g].c(qb * 512, (qb + 1) * 512, p0=p0, np_=64), y0[h2].c(0, 512, p0=p0, np_=64),
                       prm.c(P_AG + g, P_AG + g + 1, p0=p0, np_=64), ALU.mult, [y0[h2], prm], [YAP[g]])

            def epi3(qb, bp):
                for i in range(4):
                    for g_ in range(4):
                        mm(PS(bp[0], i, [[1, 1]]), ysq.c(g_ * 512 + i * 128, g_ * 512 + (i + 1) * 128),
                           onesb.c(0, 1), g_ == 0, g_ == 3, [ysq, onesb], [PB[bp[0]]])
                cp("dve", ssq_a.c(qb * 4, qb * 4 + 4), PS(bp[0], 0, [[1, 4]]), [PB[bp[0]]], [ssq_a])

            def d_tile(i, pa, pl):
                for half in range(2):
                    for g_ in range(4):
                        mm(PS(pa[half], 0, [[1, 512]]), YAP[g_].c(i * 128, (i + 1) * 128),
                           WoA.c(g_ * 1024 + half * 512, g_ * 1024 + half * 512 + 512), g_ == 0, g_ == 3,
                           [YAP[g_], WoA], [PB[pa[half]]])
                for half in range(2):
                    for cc in range(4):
                        mm(PS(pl[half], 0, [[1, 512]]), YL[cc].c(i * 128, (i + 1) * 128),
                           WoL.c(cc * 1024 + half * 512, cc * 1024 + half * 512 + 512), cc == 0, cc == 3,
                           [YL[cc], WoL], [PB[pl[half]]])
                stt(H[i].full(), PS(pa[0], 0, [[1, 1024]]), ssq_a.c(16 + i, 17 + i), H[i].full(), ALU.mult, ALU.add,
                    [PB[pa[0]], PB[pa[1]], ssq_a, H[i]], [H[i]])
                stt(H[i].full(), PS(pl[0], 0, [[1, 1024]]), ssq_l.c(16 + i, 17 + i), H[i].full(), ALU.mult, ALU.add,
                    [PB[pl[0]], PB[pl[1]], ssq_l, H[i]], [H[i]])

            NIT = 512
            todo = {}
            if stop_n >= STAGE["G"]:
                todo[3] = [lambda bp: combine(3, on_act=False)]
                todo[12] = [lambda bp: (combine_pe(3, uprs[1]), AR.release(YLQ, *HF, HB, GX[3]))]
            qk(0, sprs[0])
            qk(1, sprs[1])
            for n in range(NIT):
                qb, g, kt = n // 128, (n // 32) % 4, n % 32
                up = uprs[(n // 32) % 2]
                sp = sprs[n % 2]
                pt = PT[ptn[0] % 3]
                ptn[0] += 1
                act(pt.full(), PS(sp[0], 0, [[1, 1024]]), AF.Exp, [PB[sp[0]], PB[sp[1]], misc], [pt],
                    bias=misc.c(M_NB, M_NB + 1), scale=scale)
                if n + 2 < NIT:
                    qk(n + 2, sp)
                vx = VX[kt // 4]
                vo = (kt % 4) * 256
                for h2 in range(2):
                    mm(PS(up[h2], 0, [[1, 512]]), vx.c(vo + 128 * h2, vo + 128 * h2 + 128),
                       pt.c(512 * h2, 512 * h2 + 512), kt == 0, kt == 31, [vx, pt], [PB[up[h2]]])
                for f_ in todo.pop(n, []):
                    f_(None)
                if kt == 31:
                    epi1(up)
                    last = (n + 1 >= NIT)
                    d2 = n if last else n + 8
                    todo.setdefault(d2, []).append(lambda bp, qb=qb, g=g, up=up: epi2(qb, g, up, bp))
                    if g == 3:
                        d3 = n if last else n + 14
                        todo.setdefault(d3, []).append(lambda bp, qb=qb, up=up: epi3(qb, up))
                    if last:
                        if stop_n >= STAGE["G"]:
                            act(ssq_a.c(16, 28), ssq_a.c(0, 12), AF.Sqrt, [ssq_a], [ssq_a], scale=1.0 / 512, bias=EPS)
                            recip(ssq_a.c(16, 28), ssq_a.c(16, 28), [ssq_a], [ssq_a])
                            act(ssq_l.c(16, 32), ssq_l.c(0, 16), AF.Sqrt, [ssq_l], [ssq_l], scale=1.0 / 512, bias=EPS)
                            recip(ssq_l.c(16, 32), ssq_l.c(16, 32), [ssq_l], [ssq_l])
                            free_u = uprs[0] if up is uprs[1] else uprs[1]
                            d_tile(0, sprs[0], sprs[1])
                            d_tile(1, free_u, sprs[0])
                        for f_ in todo.pop(n, []):
                            f_(None)
            assert not todo
            AR.release(*PT, *rd, *bcs, *y0, ysq, *QT, *KT, *VX)
            if stop_n == STAGE["B"]:
                dump_src.update({"YA0": YAP[0], "YA5": YAP[2], "ssq_a": ssq_a})


        final_keys = []
        if stop_n >= STAGE["G"]:
            def next_bank():
                return next_pair()[0]

            ssq_m = AR.alloc("ssq_m", 48, F32)
            WU = [AR.alloc(f"WU{i}", 8 * 1024, BF16) for i in range(1)]
            WD = [AR.alloc(f"WD{i}", 8 * 1024, BF16) for i in range(1)]

            def load_w(qd):
                sl = qd % 2
                dma("pool", WU[sl].v(0, [[1024, 8], [1, 1024]]),
                    w_up[:, qd * 1024:(qd + 1) * 1024].rearrange("(c p) n -> p c n", p=128), f"wu{sl}", [], [WU[sl]])
                dma("pool", WD[sl].v(0, [[1024, 8], [1, 1024]]),
                    w_down[qd * 1024:(qd + 1) * 1024, :].rearrange("(j p) n -> p j n", p=128), f"wd{sl}", [], [WD[sl]])

            load_w(0)
            PBT = AR.alloc("PBT", 16 * 256, BF16)
            PTT = [AR.alloc(f"PTT{i}", 2 * 512, BF16) for i in range(4)]
            dma("pool", PBT.v(0, [[256, 16], [1, 256]]), p_own.rearrange("(t p) f -> p t f", p=128), "pbt", [], [PBT])
            junk2 = [AR.alloc(f"junk2_{i}", D, BF16) for i in range(2)]
            j2n = [0]

            def sq_accum(src, dst_t, col):
                jk = junk2[j2n[0] % 2]
                j2n[0] += 1
                act(jk.full(), src.full(), AF.Square, [src], [jk, dst_t], accum=dst_t.c(col, col + 1))
            act(ssq_a.c(28, 32), ssq_a.c(12, 16), AF.Sqrt, [ssq_a], [ssq_a], scale=1.0 / 512, bias=EPS)
            recip(ssq_a.c(28, 32), ssq_a.c(28, 32), [ssq_a], [ssq_a])
            HN = []
            xs2 = []

            def nt_xs(ssq_t, i):
                x2 = xs2[i % 2]
                act(x2.full(), H[i].full(), AF.Copy, [H[i], ssq_t], [x2], scale=ssq_t.c(32 + i, 33 + i))

            def nt_tr(i):
                x2 = xs2[i % 2]
                b = next_bank()
                for c in range(8):
                    tr(PSB(b, c * 128, [[1, 128]]), x2.c(c * 128, (c + 1) * 128), identb.full(), [x2, identb], [PB[b]])
                return b

            def nt_ev(gcol, tb, i4, b):
                tt("dve", HN[tb].v(i4 * 128, [[512, 8], [1, 128]]), PSB(b, 0, [[128, 8], [1, 128]]),
                   prm.v(gcol, [[1, 8], [0, 128]]), ALU.mult, [PB[b], prm], [HN[tb]])

            def rstd_pow(t_, src0, mid0, dst0, n_):
                ts("dve", t_.c(mid0, mid0 + n_), t_.c(src0, src0 + n_), 1.0 / D, ALU.mult, [t_], [t_], s2=EPS, op1=ALU.add)
                tt("pool", t_.c(dst0, dst0 + n_), t_.c(mid0, mid0 + n_), mhalf.c(0, n_), ALU.pow, [t_, mhalf], [t_])

            def nt_pre(ssq_t, tb):
                c0 = tb * 4
                rstd_pow(ssq_t, c0, 16 + c0, 32 + c0, 4)
                nt_xs(ssq_t, c0)
                nt_xs(ssq_t, c0 + 1)

            def nt_postA(gcol, ssq_t, tb):
                c0 = tb * 4
                b0 = nt_tr(c0)
                b1 = nt_tr(c0 + 1)
                nt_ev(gcol, tb, 0, b0)
                nt_xs(ssq_t, c0 + 2)
                nt_ev(gcol, tb, 1, b1)
                nt_xs(ssq_t, c0 + 3)

            def nt_postB(gcol, ssq_t, tb):
                c0 = tb * 4
                b2 = nt_tr(c0 + 2)
                b3 = nt_tr(c0 + 3)
                nt_ev(gcol, tb, 2, b2)
                nt_ev(gcol, tb, 3, b3)

            def nt_post(gcol, ssq_t, tb):
                nt_postA(gcol, ssq_t, tb)
                nt_postB(gcol, ssq_t, tb)

            xs2 += [AR.alloc(f"xs2_{i}", D, BF16) for i in range(2)]

            def p_transposes():
                for i_ in range(16):
                    tb_, i4_ = i_ // 4, i_ % 4
                    b = next_bank()
                    for c2 in range(2):
                        tr(PSB(b, c2 * 128, [[1, 128]]), PBT.c(i_ * 256 + c2 * 128, i_ * 256 + (c2 + 1) * 128),
                           identb.full(), [PBT, identb], [PB[b]])
                    cp("dve", PTT[tb_].v(i4_ * 128, [[512, 2], [1, 128]]), PSB(b, 0, [[128, 2], [1, 128]]),
                       [PB[b]], [PTT[tb_]])
                AR.release(PBT)

            for i in range(16):
                if i == 4:
                    nt_pre(ssq_m, 0)
                if i == 9:
                    p_transposes()
                if i >= 2:
                    d_tile(i, next_pair(), next_pair())
                sq_accum(H[i], ssq_m, i)
            AR.release(*YAP, *YL, WoA, WoL)
            HN += [AR.alloc(f"HN{i}", 8 * 512, BF16) for i in range(4)]

            nt_post(P_MLP, ssq_m, 0)
            nt_pre(ssq_m, 1)
            nt_postA(P_MLP, ssq_m, 1)
            WU.append(AR.alloc("WU1", 8 * 1024, BF16))
            WD.append(AR.alloc("WD1", 8 * 1024, BF16))
            load_w(1)
            AT = [[AR.alloc(f"AT{i}_{h}", 4 * 512, BF16) for h in range(2)] for i in range(2)]
            rl = [AR.alloc(f"rl{i}", 512, F32) for i in range(2)]
            rln = [0]
            steps = [(qd, tb) for qd in range(4) for tb in range(4)]

            def up(n):
                qd, tb = steps[n]
                sl = qd % 2
                at = AT[n % 2]
                for j in range(8):
                    b = next_bank()
                    for c in range(8):
                        mm(PS(b, 0, [[1, 512]]), WU[sl].c(c * 1024 + j * 128, c * 1024 + (j + 1) * 128),
                           HN[tb].c(c * 512, (c + 1) * 512), c == 0, c == 7, [WU[sl], HN[tb]], [PB[b]])
                    r_ = rl[rln[0] % 2]
                    rln[0] += 1
                    act(r_.full(), PS(b, 0, [[1, 512]]), AF.Relu, [PB[b]], [r_])
                    tt("dve", at[j // 4].c((j % 4) * 512, (j % 4 + 1) * 512), r_.full(), r_.full(), ALU.mult, [r_], [at[j // 4]])

            def down(n):
                qd, tb = steps[n]
                sl = qd % 2
                at = AT[n % 2]
                for i4 in range(4):
                    i = tb * 4 + i4
                    pr = next_pair()
                    for half in range(2):
                        for j in range(8):
                            mm(PS(pr[half], 0, [[1, 512]]),
                               at[j // 4].c((j % 4) * 512 + i4 * 128, (j % 4) * 512 + (i4 + 1) * 128),
                               WD[sl].c(j * 1024 + half * 512, j * 1024 + half * 512 + 512), j == 0, j == 7,
                               [at[j // 4], WD[sl]], [PB[pr[half]]])
                    tt("dve", H[i].full(), H[i].full(), PS(pr[0], 0, [[1, 1024]]), ALU.add,
                       [H[i], PB[pr[0]], PB[pr[1]]], [H[i]])

            ssq_p = AR.alloc("ssq_p", 48, F32)
            fs = AR.alloc("fs", 48, F32)
            Fst = {}

            def ple_setup():
                Fst["WG"] = AR.alloc("WG", 8 * 1024, BF16)
                Fst["WP"] = AR.alloc("WP", 2 * 1024, BF16)
                Fst["fg"] = AR.alloc("fg", D, F32)
                Fst["gate"] = [AR.alloc("gate0", D, F32)]
                Fst["ot"] = [AR.alloc(f"ot{i}", D, F32) for i in range(2)]
                dma("pool", Fst["WG"].v(0, [[1024, 8], [1, 1024]]), w_gate.rearrange("(c p) n -> p c n", p=128),
                    "wg", [], [Fst["WG"]])
                dma("pool", Fst["WP"].v(0, [[1024, 2], [1, 1024]]), w_proj.rearrange("(c p) n -> p c n", p=128),
                    "wg", [], [Fst["WP"]])
                dma("sp", Fst["fg"].full(), fing.partition_broadcast(128), "fg", [], [Fst["fg"]])

            def ple_block(tb):
                WG, WP, fg = Fst["WG"], Fst["WP"], Fst["fg"]
                if tb < 3:
                    nt_postB(P_PLE, ssq_p, tb)
                if tb + 1 < 4:
                    nt_pre(ssq_p, tb + 1)
                if tb == 2:
                    nt_postA(P_PLE, ssq_p, 3)
                    nt_postB(P_PLE, ssq_p, 3)
                for i4 in range(4):
                    i = tb * 4 + i4
                    pg = next_pair()
                    pp = next_pair()
                    for half in range(2):
                        for c in range(8):
                            mm(PS(pg[half], 0, [[1, 512]]), HN[tb].c(c * 512 + i4 * 128, c * 512 + (i4 + 1) * 128),
                               WG.c(c * 1024 + half * 512, c * 1024 + half * 512 + 512), c == 0, c == 7,
                               [HN[tb], WG], [PB[pg[half]]])
                    for half in range(2):
                        for c2 in range(2):
                            mm(PS(pp[half], 0, [[1, 512]]), PTT[tb].c(c2 * 512 + i4 * 128, c2 * 512 + (i4 + 1) * 128),
                               WP.c(c2 * 1024 + half * 512, c2 * 1024 + half * 512 + 512), c2 == 0, c2 == 1,
                               [PTT[tb], WP], [PB[pp[half]]])
                    gt = Fst["gate"][0]
                    act(gt.full(), PS(pg[0], 0, [[1, 1024]]), AF.Tanh, [PB[pg[0]], PB[pg[1]]], [gt], scale=0.5)
                    stt(gt.full(), gt.full(), 1.0, PS(pp[0], 0, [[1, 1024]]), ALU.add, ALU.mult,
                        [gt, PB[pp[0]], PB[pp[1]]], [gt])
                    stt(H[i].full(), gt.full(), 0.5, H[i].full(), ALU.mult, ALU.add, [gt, H[i]], [H[i]])
                    sq_accum(H[i], fs, i)
                    rstd_pow(fs, i, 16 + i, 32 + i, 1)
                    o_ = Fst["ot"][i % 2]
                    stt(o_.full(), H[i].full(), fs.c(32 + i, 33 + i), fg.full(), ALU.mult, ALU.mult, [H[i], fs, fg], [o_])
                    dma("sp", out_d[i * 128:(i + 1) * 128, :], o_.full(), f"out{i % 2}", [o_], [])

            up(0)
            nt_postB(P_MLP, ssq_m, 1)
            nt_pre(ssq_m, 2)
            for n in range(16):
                qd, tb = steps[n]
                if qd == 3 and tb == 1:
                    nt_pre(ssq_p, 0)
                if n + 1 < 16:
                    up(n + 1)
                if n in (0, 1):
                    nt_postA(P_MLP, ssq_m, n + 2)
                if qd == 3 and tb >= 1:
                    nt_postA(P_PLE, ssq_p, tb - 1)
                down(n)
                if n in (0, 1):
                    nt_postB(P_MLP, ssq_m, n + 2)
                    if n == 0:
                        nt_pre(ssq_m, 3)
                if tb == 3 and qd + 2 < 4:
                    load_w(qd + 2)
                if qd == 3:
                    if tb == 0:
                        AR.release(WU[0], WD[0])
                        ple_setup()
                    for i4 in range(4):
                        sq_accum(H[tb * 4 + i4], ssq_p, tb * 4 + i4)
                    if tb >= 1:
                        ple_block(tb - 1)
            ple_block(3)
            final_keys += ["out0", "out1"]

        if dumps:
            stg = AR.alloc("dump_stage", 4100, F32)
            for n_, (nm, ncols) in enumerate(dumps):
                src = dump_src[nm]
                cp("dve", stg.c(0, ncols), src.c(0, ncols), [src], [stg])
                dma("sp", dump_d[nm][:, :], stg.c(0, ncols), "dump", [stg], [])
            final_keys.append("dump")

        print("arena peak bytes:", AR.peak, "ops:", len(S.ops))
        S.finalize()
        keys = sorted(S.dma_count.keys())
        with ExitStack() as es2:
            engsem = {e: es2.enter_context(nc.semaphore("sem_" + e)) for e in Sched.ENGS}
            dmasem = {k: es2.enter_context(nc.semaphore("dma_" + k)) for k in keys}
            block = es2.enter_context(nc.Block())
            S.emit(nc, block, engsem, dmasem, final_keys)
    return nc


def _rope_tables_local(t):
    rows = S_FULL // 64
    pos = np.arange(S_FULL)
    row = (pos // 64).astype(np.float32)
    col = (pos % 64).astype(np.float32)
    inv = (np.float32(10000.0) ** (-np.arange(16, dtype=np.float32) / np.float32(16))).astype(np.float32)
    ang_r = row[:, None] * inv[None, :]
    ang_c = col[:, None] * inv[None, :]
    cs = np.stack([np.cos(ang_r), np.cos(ang_c)], axis=1).astype(np.float32)
    sn = np.stack([np.sin(ang_r), np.sin(ang_c)], axis=1).astype(np.float32)
    C = np.concatenate([cs.reshape(S_FULL, 32), cs.reshape(S_FULL, 32)], axis=1)
    Sg = np.concatenate([-sn.reshape(S_FULL, 32), sn.reshape(S_FULL, 32)], axis=1)
    if t == 0:
        order = np.arange(S_FULL)
    else:
        order = np.concatenate([np.arange(4095, 2047, -1), np.arange(2047, -1, -1)])
    return np.ascontiguousarray(C[order]), np.ascontiguousarray(Sg[order])


def _perm_sai(v64):
    v = np.asarray(v64).reshape(2, 2, 16)
    perm = v.transpose(1, 0, 2).reshape(64)
    swap = v[:, ::-1, :].transpose(1, 0, 2).reshape(64)
    return perm, swap


def prep_inputs(inp):
    f = lambda a: np.ascontiguousarray(np.asarray(a, dtype=np.float32))
    x = f(inp["x"])
    p = f(inp["p"])[0]
    w_in = f(inp["w_in"])[0]
    qcols = np.zeros(512, np.int64)
    for s in range(2):
        for hp in range(8):
            head = (hp % 2) * 4 + hp // 2
            for a in range(2):
                for i in range(16):
                    qcols[s * 256 + hp * 32 + a * 16 + i] = head * 64 + a * 32 + s * 16 + i
    kcols = np.zeros(128, np.int64)
    for s in range(2):
        for h in range(2):
            for a in range(2):
                for i in range(16):
                    kcols[s * 64 + h * 32 + a * 16 + i] = 512 + h * 64 + a * 32 + s * 16 + i
    cols = np.concatenate([qcols, kcols, np.arange(640, D_IN)])
    w_in_p = np.ascontiguousarray(w_in[:, cols])
    gqp, gqs = _perm_sai(f(inp["q_norm"])[0])
    gkp, gks = _perm_sai(f(inp["k_norm"])[0])
    gqk = np.concatenate([gqp, gqs, gkp, gks])[None, :].astype(np.float32)

    def pc(v, n):
        return np.asarray(v, np.float32).reshape(n, 128).T

    conv_w = f(inp["conv_w"])[0]
    attn_g = f(inp["attn_out_norm"])[0].reshape(8, 64)
    ag = np.zeros((128, 8), np.float32)
    for hh in range(8):
        head = (hh % 2) * 4 + hh // 2
        ag[(hh % 2) * 64:(hh % 2) * 64 + 64, hh // 2] = attn_g[head]
    w_out = f(inp["w_out"])[0]
    wo_rows = np.concatenate([np.arange(((hh % 2) * 4 + hh // 2) * 64, ((hh % 2) * 4 + hh // 2) * 64 + 64)
                              for hh in range(8)] + [np.arange(512, 1024)])
    w_out_p = np.ascontiguousarray(w_out[wo_rows])
    common = {
        "w_in": w_in_p, "gqk": gqk, "w_out": w_out_p, "w_up": f(inp["w_up"])[0], "w_down": f(inp["w_down"])[0],
        "w_gate": f(inp["w_ple_gate"])[0], "w_proj": f(inp["w_ple_proj"])[0], "fing": f(inp["final_norm"])[None, :],
    }
    rope = {t: _rope_tables_local(t) for t in range(2)}
    per_t = {}
    for t in range(2):
        dirs = [0, 1] if t == 0 else [1, 0]
        taps5 = np.zeros((5, 512), np.float32)
        if t == 0:
            taps5[0:4] = conv_w
        else:
            taps5[1:5] = conv_w[::-1]
        prm = np.zeros((128, 96), np.float32)
        prm[:, 0:8] = pc(f(inp["mix_norm"])[0], 8)
        prm[:, 8:16] = pc(f(inp["mlp_norm"])[0], 8)
        prm[:, 16:24] = pc(f(inp["ple_norm"])[0], 8)
        for cc in range(4):
            for j in range(5):
                prm[:, 24 + cc * 5 + j] = taps5[j, cc * 128:(cc + 1) * 128]
        prm[:, 44:48] = pc(f(inp["conv_b"])[0], 4)
        for e in range(2):
            prm[:, 48 + e * 4:48 + e * 4 + 4] = pc(f(inp["lru_ba"])[0, dirs[e]], 4)
            prm[:, 56 + e * 4:56 + e * 4 + 4] = pc(f(inp["lru_bx"])[0, dirs[e]], 4)
            prm[:, 64 + e * 4:64 + e * 4 + 4] = pc(f(inp["lru_lambda"])[0, dirs[e]], 4)
        prm[:, 72:80] = ag
        prm[:, 80:84] = pc(f(inp["lru_out_norm"])[0], 4)
        wa = f(inp["lru_wa"])[0]
        wx = f(inp["lru_wx"])[0]
        wbd = np.zeros((16, 128, 128), np.float32)
        for e in range(2):
            for ty, wsrc in enumerate((wa, wx)):
                for cc in range(4):
                    k = (e * 2 + ty) * 4 + cc
                    wbd[k, 0:64, 0:64] = wsrc[dirs[e], 2 * cc]
                    wbd[k, 64:128, 64:128] = wsrc[dirs[e], 2 * cc + 1]
        per_t[t] = {"prm": prm, "wbd": wbd.reshape(16 * 128, 128), "ropeC": rope[t][0], "ropeS": rope[t][1]}
    in_maps = []
    for c in range(N_CORES):
        b, t = c // 2, c % 2
        if t == 0:
            xo, xh, po = x[b, :TOK], x[b, TOK:], p[b, :TOK]
        else:
            xo, xh, po = x[b, TOK:][::-1], x[b, :TOK][::-1], p[b, TOK:][::-1]
        m = dict(common)
        m.update(per_t[t])
        m["x_own"] = np.ascontiguousarray(xo)
        m["x_oth"] = np.ascontiguousarray(xh)
        m["p_own"] = np.ascontiguousarray(po)
        in_maps.append(m)
    return in_maps


_NC_CACHE = {}


def kernel(**inputs):
    in_maps = prep_inputs(inputs)
    if "nc" not in _NC_CACHE:
        _NC_CACHE["nc"] = build_program()
    nc = _NC_CACHE["nc"]
    res = run_bass_kernel_spmd(nc, in_maps, core_ids=list(range(N_CORES)))
    out = np.zeros((4, S_FULL, D), np.float32)
    for c in range(N_CORES):
        b, t = c // 2, c % 2
        o = np.asarray(res.results[c]["out"], np.float32)
        if t == 0:
            out[b, :TOK] = o
        else:
            out[b, TOK:] = o[::-1]
    return out
```

```python
import numpy as np
import concourse.bass as bass
import concourse.mybir as mybir
from concourse.bass_utils import run_bass_kernel_spmd

F32 = mybir.dt.float32
BF16 = mybir.dt.bfloat16
ALU = mybir.AluOpType
AF = mybir.ActivationFunctionType
AX = mybir.AxisListType

D = 1024
S_FULL = 4096
TOK = 2048
D_IN = 1792
D_FF = 4096
EPS = 1e-6
N_CORES = 8


class Buf:
    __slots__ = ("name", "writers", "readers")

    def __init__(self, name):
        self.name = name
        self.writers = {}
        self.readers = {}


class Op:
    __slots__ = ("idx", "eng", "fn", "deps", "signal", "sigval", "dma_key", "dma_val", "dma_waits", "eidx")

    def __init__(self, idx, eng, fn):
        self.idx = idx
        self.eng = eng
        self.fn = fn
        self.deps = {}
        self.signal = False
        self.sigval = 0
        self.dma_key = None
        self.dma_val = 0
        self.dma_waits = {}
        self.eidx = 0


class Sched:
    ENGS = ("pe", "act", "dve", "pool", "sp")
    WIN = {"pe": 0, "act": 2, "dve": 2, "pool": 8, "sp": 0}

    def __init__(self):
        self.ops = []
        self.dma_count = {}
        self.dma_consumed = {}

    def op(self, eng, fn, reads=(), writes=(), dma_key=None):
        o = Op(len(self.ops), eng, fn)
        me = ("dma", dma_key) if dma_key is not None else eng
        deps = o.deps
        for b in reads:
            for w in b.writers.values():
                deps[w] = True
        for b in writes:
            for w in b.writers.values():
                deps.setdefault(w, False)
            for r in b.readers.values():
                deps.setdefault(r, False)
        for d in list(deps):
            if d.dma_key is not None:
                k = d.dma_key
                o.dma_waits[k] = self.dma_count[k] * 16
                self.dma_consumed[k] = True
                del deps[d]
        if dma_key is not None:
            o.dma_key = dma_key
            c = self.dma_count.get(dma_key, 0)
            if c > 0 and self.dma_consumed.get(dma_key, False):
                o.dma_waits[dma_key] = max(o.dma_waits.get(dma_key, 0), c * 16)
            self.dma_consumed[dma_key] = False
            self.dma_count[dma_key] = c + 1
            o.dma_val = (c + 1) * 16
        for b in reads:
            b.readers[me] = o
        for b in writes:
            b.writers[me] = o
        self.ops.append(o)
        return o

    @staticmethod
    def _needs_sync(o, d, raw):
        if d.eng == o.eng and o.dma_key is None:
            if o.eng == "pe":
                return False
            if not raw and (o.eidx - d.eidx) > Sched.WIN[o.eng]:
                return False
        return True

    def finalize(self):
        cnt_e = {e: 0 for e in self.ENGS}
        for o in self.ops:
            if o.dma_key is None:
                cnt_e[o.eng] += 1
                o.eidx = cnt_e[o.eng]
        for o in self.ops:
            for d, raw in o.deps.items():
                if self._needs_sync(o, d, raw):
                    d.signal = True
        cnt = {e: 0 for e in self.ENGS}
        for o in self.ops:
            if o.dma_key is None and o.signal:
                cnt[o.eng] += 1
                o.sigval = cnt[o.eng]

    def emit(self, nc, block, engsem, dmasem, final_keys):
        streams = {e: [o for o in self.ops if o.eng == e] for e in self.ENGS}

        def run(e, eng):
            waited = {}
            for o in streams[e]:
                need = {}
                for d, raw in o.deps.items():
                    if not self._needs_sync(o, d, raw):
                        continue
                    s = engsem[d.eng]
                    if need.get(s, (0,))[0] < d.sigval:
                        need[s] = (d.sigval, s)
                for k, v in o.dma_waits.items():
                    s = dmasem[k]
                    if need.get(s, (0,))[0] < v:
                        need[s] = (v, s)
                for s, (v, _) in need.items():
                    if waited.get(s, 0) < v:
                        eng.wait_ge(s, v)
                        waited[s] = v
                inst = o.fn(eng)
                if o.dma_key is not None:
                    inst.then_inc(dmasem[o.dma_key], 16)
                elif o.signal:
                    inst.then_inc(engsem[o.eng], 1)
            if e == "sp":
                for k in final_keys:
                    eng.wait_ge(dmasem[k], self.dma_count[k] * 16)

        @block.sync
        def _(eng):
            run("sp", eng)

        @block.gpsimd
        def _(eng):
            run("pool", eng)

        @block.tensor
        def _(eng):
            run("pe", eng)

        @block.vector
        def _(eng):
            run("dve", eng)

        @block.scalar
        def _(eng):
            run("act", eng)


class T:
    __slots__ = ("base", "pstep", "off", "ncols", "buf", "start", "end", "dt")

    def __init__(self, base_ap, off, ncols, buf, start=0, end=0, dt=None):
        self.base = base_ap
        self.pstep = base_ap.ap[0][0]
        self.off = off
        self.ncols = ncols
        self.buf = buf
        self.start = start
        self.end = end
        self.dt = dt

    def v(self, off, dims, p0=0, np_=128):
        return bass.AP(self.base.tensor, p0 * self.pstep + self.off + off,
                       [[self.pstep, np_]] + [list(d) for d in dims])

    def c(self, lo, hi, p0=0, np_=128):
        return self.v(lo, [[1, hi - lo]], p0, np_)

    def full(self):
        return self.v(0, [[1, self.ncols]])


class Arena:
    def __init__(self, ap_f32, ap_bf16, nbytes):
        self.views = {F32: ap_f32, BF16: ap_bf16}
        self.free = [(0, nbytes)]
        self.dead = []
        self.peak = 0

    def alloc(self, name, ncols, dt):
        esz = 4 if dt == F32 else 2
        nbytes = (ncols * esz + 63) // 64 * 64
        for i, (s, e) in enumerate(self.free):
            if e - s >= nbytes:
                start = s
                if e - s == nbytes:
                    self.free.pop(i)
                else:
                    self.free[i] = (s + nbytes, e)
                break
        else:
            raise RuntimeError(f"arena OOM allocating {name} ({nbytes} B); free={self.free}")
        end = start + nbytes
        self.peak = max(self.peak, end)
        b = Buf(name)
        for (ds, de, db) in self.dead:
            if ds < end and de > start:
                for k, o in db.writers.items():
                    if k not in b.writers or b.writers[k].idx < o.idx:
                        b.writers[k] = o
                for k, o in db.readers.items():
                    if k not in b.readers or b.readers[k].idx < o.idx:
                        b.readers[k] = o
        return T(self.views[dt], start // esz, ncols, b, start, end, dt)

    def release(self, *tiles):
        for t in tiles:
            self.dead.append((t.start, t.end, t.buf))
            self.free.append((t.start, t.end))
        self.free.sort()
        merged = []
        for s, e in self.free:
            if merged and merged[-1][1] == s:
                merged[-1] = (merged[-1][0], e)
            else:
                merged.append((s, e))
        self.free = merged


STAGE = {"A": 0, "C": 1, "B": 2, "G": 3}


def build_program(stop_after="G", dumps=()):
    nc = bass.Bass("TRN2", target_bir_lowering=False)
    stop_n = STAGE[stop_after]
    S = Sched()

    def din(name, shape):
        return nc.dram_tensor(name, list(shape), F32, kind="ExternalInput").ap()

    x_own = din("x_own", [TOK, D])
    x_oth = din("x_oth", [TOK, D])
    p_own = din("p_own", [TOK, 256])
    w_in = din("w_in", [D, D_IN])
    ropeC = din("ropeC", [S_FULL, 64])
    ropeS = din("ropeS", [S_FULL, 64])
    gqk = din("gqk", [1, 256])
    prm_d = din("prm", [128, 96])
    wbd_d = din("wbd", [16 * 128, 128])
    w_out = din("w_out", [D, D])
    w_up = din("w_up", [D, D_FF])
    w_down = din("w_down", [D_FF, D])
    w_gate = din("w_gate", [D, D])
    w_proj = din("w_proj", [256, D])
    fing = din("fing", [1, D])
    mixg_row = din("mixg_row", [1, D])
    out_d = nc.dram_tensor("out", [TOK, D], F32, kind="ExternalOutput").ap()
    dump_d = {}
    for (nm, ncols) in dumps:
        dump_d[nm] = nc.dram_tensor("dbg_" + nm, [128, ncols], F32, kind="ExternalOutput").ap()

    AW = 52736
    from contextlib import ExitStack
    with ExitStack() as es:
        arena_t = es.enter_context(nc.sbuf_tensor("arena", [128, AW], F32))
        ps_t = es.enter_context(nc.psum_tensor("ps", [128, 4096], F32))
        a_f32 = arena_t[:, :]
        a_bf = a_f32.bitcast(BF16)
        AR = Arena(a_f32, a_bf, AW * 4)
        ps_f32 = ps_t[:, :]
        ps_bf = ps_f32.bitcast(BF16)
        PB = [Buf(f"psum{i}") for i in range(8)]

        def PS(bank, off, dims, p0=0, np_=128):
            pstep = ps_f32.ap[0][0]
            return bass.AP(ps_f32.tensor, p0 * pstep + bank * 512 + off,
                           [[pstep, np_]] + [list(d) for d in dims])

        def PSB(bank, off, dims, p0=0, np_=128):
            pstep = ps_bf.ap[0][0]
            return bass.AP(ps_bf.tensor, p0 * pstep + bank * 1024 + off,
                           [[pstep, np_]] + [list(d) for d in dims])

        def bufs(xs):
            out = []
            for x in xs:
                out.append(x.buf if isinstance(x, T) else x)
            return out

        def mm(out, lhsT, rhs, start, stop, rd, wr):
            S.op("pe", lambda e: e.matmul(out, lhsT=lhsT, rhs=rhs, start=start, stop=stop),
                 bufs(rd), bufs(wr))

        def tr(out, in_, ident, rd, wr):
            S.op("pe", lambda e: e.transpose(out=out, in_=in_, identity=ident), bufs(rd), bufs(wr))

        def act(out, in_, func, rd, wr, bias=None, scale=None, accum=None):
            kw = {}
            if bias is not None:
                kw["bias"] = bias
            if scale is not None:
                kw["scale"] = scale
            if accum is not None:
                kw["accum_out"] = accum
            S.op("act", lambda e: e.activation(out=out, in_=in_, func=func, **kw), bufs(rd), bufs(wr))

        def tt(eng, out, in0, in1, op, rd, wr):
            S.op(eng, lambda e: e.tensor_tensor(out=out, in0=in0, in1=in1, op=op), bufs(rd), bufs(wr))

        def ts(eng, out, in0, s1, op0, rd, wr, s2=None, op1=None):
            if op1 is None:
                S.op(eng, lambda e: e.tensor_scalar(out=out, in0=in0, scalar1=s1, scalar2=None, op0=op0),
                     bufs(rd), bufs(wr))
            else:
                S.op(eng, lambda e: e.tensor_scalar(out=out, in0=in0, scalar1=s1, scalar2=s2, op0=op0, op1=op1),
                     bufs(rd), bufs(wr))

        def stt(out, in0, scalar, in1, op0, op1, rd, wr):
            S.op("dve", lambda e: e.scalar_tensor_tensor(out=out, in0=in0, scalar=scalar, in1=in1, op0=op0, op1=op1),
                 bufs(rd), bufs(wr))

        def cp(eng, out, in_, rd, wr):
            S.op(eng, lambda e: e.tensor_copy(out=out, in_=in_), bufs(rd), bufs(wr))

        def recip(out, in_, rd, wr):
            S.op("dve", lambda e: e.reciprocal(out=out, in_=in_), bufs(rd), bufs(wr))

        def reduce_(out, in_, axis, op, rd, wr, absv=None):
            if absv:
                S.op("dve", lambda e: e.tensor_reduce(out=out, in_=in_, axis=axis, op=op, apply_absolute_value=True),
                     bufs(rd), bufs(wr))
            else:
                S.op("dve", lambda e: e.tensor_reduce(out=out, in_=in_, axis=axis, op=op), bufs(rd), bufs(wr))

        def memset(eng, ap, val, wr):
            S.op(eng, lambda e: e.memset(ap, val), [], bufs(wr))

        def dma(q, out, in_, key, rd, wr):
            S.op(q, lambda e: e.dma_start(out=out, in_=in_), bufs(rd), bufs(wr), dma_key=key)

        def scan(out, d0, d1, init, rd, wr):
            S.op("dve", lambda e: e.tensor_tensor_scan(out=out, data0=d0, data1=d1, initial=init,
                                                       op0=ALU.mult, op1=ALU.add), bufs(rd), bufs(wr))

        identf = AR.alloc("identf", 128, F32)
        identb = AR.alloc("identb", 128, BF16)
        onesf = AR.alloc("onesf", 64, F32)
        onesb = AR.alloc("onesb", 2, BF16)
        prm = AR.alloc("prm", 96, F32)
        gq = AR.alloc("gq", 256, F32)
        misc = AR.alloc("misc", 96, F32)
        P_MIX, P_MLP, P_PLE, P_CONV, P_CONVB, P_LBA, P_LBX, P_LAM, P_AG, P_LG = 0, 8, 16, 24, 44, 48, 56, 64, 72, 80
        M_CL, M_CL2, M_NB, M_TMP, M_THI, M_TLO, M_HBA, M_HBX, M_HCL = 0, 8, 16, 17, 24, 44, 64, 72, 80

        memset("pool", identf.full(), 0.0, [identf])
        memset("pool", onesf.full(), 1.0, [onesf])
        memset("pool", onesb.full(), 1.0, [onesb])
        mhalf = AR.alloc("mhalf", 16, F32)
        memset("pool", mhalf.full(), -0.5, [mhalf])
        S.op("pool", lambda e: e.affine_select(out=identf.full(), in_=identf.full(), pattern=[[-1, 128]],
                                               compare_op=ALU.not_equal, fill=1.0, base=0, channel_multiplier=1),
             [identf.buf], [identf.buf])
        cp("pool", identb.full(), identf.full(), [identf], [identb])
        dma("sp", prm.full(), prm_d[:, :], "prm", [], [prm])
        dma("sp", gq.full(), gqk.partition_broadcast(128), "prm", [], [gq])

        act(misc.c(M_TMP, M_TMP + 8), prm.c(P_LAM, P_LAM + 8), AF.Exp, [prm], [misc], scale=-1.0)
        act(misc.c(M_TMP, M_TMP + 8), misc.c(M_TMP, M_TMP + 8), AF.Ln, [misc], [misc], bias=1.0)
        ts("dve", misc.c(M_CL, M_CL + 8), misc.c(M_TMP, M_TMP + 8), -8.0, ALU.mult, [misc], [misc])
        ts("dve", misc.c(M_CL2, M_CL2 + 8), misc.c(M_TMP, M_TMP + 8), -16.0, ALU.mult, [misc], [misc])
        ts("dve", misc.c(M_HCL, M_HCL + 8), misc.c(M_TMP, M_TMP + 8), -4.0, ALU.mult, [misc], [misc])
        ts("dve", misc.c(M_HBA, M_HBA + 8), prm.c(P_LBA, P_LBA + 8), 0.5, ALU.mult, [prm], [misc])
        ts("dve", misc.c(M_HBX, M_HBX + 8), prm.c(P_LBX, P_LBX + 8), 0.5, ALU.mult, [prm], [misc])
        reduce_(misc.c(M_TMP, M_TMP + 1), gq.c(0, 64), AX.X, ALU.max, [gq], [misc], absv=True)
        reduce_(misc.c(M_TMP + 1, M_TMP + 2), gq.c(128, 192), AX.X, ALU.max, [gq], [misc], absv=True)
        tt("dve", misc.c(M_NB, M_NB + 1), misc.c(M_TMP, M_TMP + 1), misc.c(M_TMP + 1, M_TMP + 2), ALU.mult, [misc], [misc])
        ts("dve", misc.c(M_NB, M_NB + 1), misc.c(M_NB, M_NB + 1), -8.0, ALU.mult, [misc], [misc])
        thb = AR.alloc("thb", 20, BF16)
        cp("dve", thb.full(), prm.c(P_CONV, P_CONV + 20), [prm], [thb])
        cp("dve", misc.c(M_THI, M_THI + 20), thb.full(), [thb], [misc])
        tt("dve", misc.c(M_TLO, M_TLO + 20), prm.c(P_CONV, P_CONV + 20), misc.c(M_THI, M_THI + 20), ALU.subtract,
           [prm, misc], [misc])
        QT = [AR.alloc(f"QT{i}", 4 * 512, BF16) for i in range(4)]
        KT = [AR.alloc(f"KT{i}", 512, BF16) for i in range(8)]
        VX = [AR.alloc(f"VX{i}", 4 * 256, BF16) for i in range(8)]
        XR = [AR.alloc(f"XR{i}", 4100, BF16) for i in range(4)]
        GX = [AR.alloc(f"GX{i}", TOK, BF16) for i in range(4)]
        for i in range(8):
            memset("pool", VX[i].v(64, [[256, 4], [1, 128]]), 1.0, [VX[i]])
        for i in range(4):
            memset("pool", XR[i].c(0, 2), 0.0, [XR[i]])
            memset("pool", XR[i].c(4098, 4100), 0.0, [XR[i]])

        winq = AR.alloc("winq", 8 * 768, BF16)
        winf = AR.alloc("winf", 8 * 1024, BF16)
        w_in_v = w_in.rearrange("(c p) n -> p c n", p=128)
        for c in range(8):
            dma("pool", winf.v(c * 1024, [[1, 1024]]), w_in_v[:, c, 768:D_IN], "winf", [], [winf])
        xt = [AR.alloc(f"xt{i}", D, F32) for i in range(5)]
        mgb = AR.alloc("mgb", D, F32)
        dma("sp", mgb.full(), mixg_row.partition_broadcast(128), "prm", [], [mgb])
        xs = [AR.alloc(f"xs{i}", D, BF16) for i in range(2)]
        junk = [AR.alloc(f"junk{i}", D, BF16) for i in range(2)]
        jn = [0]
        hnT = [AR.alloc(f"hnT{i}", 8 * 512, BF16) for i in range(3)]
        ssqA = [AR.alloc(f"ssqA{i}", 8, F32) for i in range(2)]
        tabC = [AR.alloc(f"tabC{i}", 256, F32) for i in range(2)]
        tabS = [AR.alloc(f"tabS{i}", 256, F32) for i in range(2)]
        GCq = [AR.alloc(f"GCq{i}", 256, F32) for i in range(2)]
        GSq = [AR.alloc(f"GSq{i}", 256, F32) for i in range(2)]
        GCk = [AR.alloc(f"GCk{i}", 256, F32) for i in range(2)]
        GSk = [AR.alloc(f"GSk{i}", 256, F32) for i in range(2)]
        SCR = [[AR.alloc(f"sq{k}", 640, F32), AR.alloc(f"qn{k}", 640, F32), AR.alloc(f"t2{k}", 640, F32),
                AR.alloc(f"qr{k}", 640, BF16), AR.alloc(f"sst{k}", 32, F32)] for k in range(2)]

        trb = [6, 7]
        trn = [0]

        def next_trb():
            b = trb[trn[0] % 2]
            trn[0] += 1
            return b

        def stage1_tabs(blk):
            own = blk < 4
            sl = blk % 2
            gl = blk * 512
            dma("sp", tabC[sl].v(0, [[64, 4], [1, 64]]), ropeC[gl:gl + 512, :].rearrange("(t p) f -> p t f", p=128),
                f"tab{sl}", [], [tabC[sl]])
            dma("sp", tabS[sl].v(0, [[64, 4], [1, 64]]), ropeS[gl:gl + 512, :].rearrange("(t p) f -> p t f", p=128),
                f"tab{sl}", [], [tabS[sl]])
            if own:
                tt("dve", GCq[sl].v(0, [[64, 4], [1, 64]]), tabC[sl].v(0, [[64, 4], [1, 64]]),
                   gq.v(0, [[0, 4], [1, 64]]), ALU.mult, [tabC[sl], gq], [GCq[sl]])
                tt("dve", GSq[sl].v(0, [[64, 4], [1, 64]]), tabS[sl].v(0, [[64, 4], [1, 64]]),
                   gq.v(64, [[0, 4], [1, 64]]), ALU.mult, [tabS[sl], gq], [GSq[sl]])
            tt("dve", GCk[sl].v(0, [[64, 4], [1, 64]]), tabC[sl].v(0, [[64, 4], [1, 64]]),
               gq.v(128, [[0, 4], [1, 64]]), ALU.mult, [tabC[sl], gq], [GCk[sl]])
            tt("dve", GSk[sl].v(0, [[64, 4], [1, 64]]), tabS[sl].v(0, [[64, 4], [1, 64]]),
               gq.v(192, [[0, 4], [1, 64]]), ALU.mult, [tabS[sl], gq], [GSk[sl]])

        def stage1_load(blk, i):
            own = blk < 4
            src = x_own if own else x_oth
            r0 = (blk % 4) * 512
            tile = blk * 4 + i
            x_t = xt[tile % 5]
            dma("sp", x_t.full(), src[r0 + i * 128:r0 + (i + 1) * 128, :], f"xt{tile % 5}", [], [x_t])

        def stage1_pre(blk, i):
            sl = blk % 2
            tile = blk * 4 + i
            x_t = xt[tile % 5]
            jk = junk[jn[0] % 2]
            jn[0] += 1
            sa = ssqA[tile % 2]
            act(jk.full(), x_t.full(), AF.Square, [x_t], [jk, sa], accum=sa.c(0, 1))
            act(sa.c(1, 2), sa.c(0, 1), AF.Sqrt, [sa], [sa], scale=1.0 / D, bias=EPS)
            recip(sa.c(2, 3), sa.c(1, 2), [sa], [sa])
            xs_t = xs[tile % 2]
            stt(xs_t.full(), x_t.full(), sa.c(2, 3), mgb.full(), ALU.mult, ALU.mult, [x_t, sa, mgb], [xs_t])

        def stage1_post(blk, i):
            sl = blk % 2
            tile = blk * 4 + i
            xs_t = xs[tile % 2]
            b = next_trb()
            for c in range(8):
                tr(PSB(b, c * 128, [[1, 128]]), xs_t.c(c * 128, (c + 1) * 128), identb.full(),
                   [xs_t, identb], [PB[b]])
            act(hnT[blk % 3].v(i * 128, [[512, 8], [1, 128]]), PSB(b, 0, [[128, 8], [1, 128]]), AF.Copy,
                [PB[b]], [hnT[blk % 3]])

        qkv_pairs = [(0, 1), (2, 3)]
        qkvn = [0]
        fmb = [4, 5]
        fmn = [0]
        pending = []

        def post_part1(blk, i, b0, own, k):
            sl = blk % 2
            sq, qn, t2, qr, sst = SCR[k]
            t1 = sq
            R = [PB[b0], PB[b0 + 1]] if own else [PB[b0]]
            ko = 512 if own else 0
            vo = ko + 128
            W = 640 if own else 128
            so = 0 if own else 512
            act(sq.c(so, 640), PS(b0, 0, [[1, W]]), AF.Square, R, [sq])
            act(VX[blk].v(i * 256, [[192, 2], [1, 64]]), PS(b0, vo, [[64, 2], [1, 64]]), AF.Copy, R, [VX[blk]])
            if own:
                reduce_(sst.c(0, 8), sq.v(0, [[32, 8], [256, 2], [1, 32]]), AX.XY, ALU.add, [sq], [sst])
            reduce_(sst.c(8, 10), sq.v(512, [[32, 2], [64, 2], [1, 32]]), AX.XY, ALU.add, [sq], [sst])
            lo = 0 if own else 8
            act(sst.c(10 + lo, 20), sst.c(lo, 10), AF.Sqrt, [sst], [sst], scale=1.0 / 64, bias=EPS)
            recip(sst.c(20 + lo, 30), sst.c(10 + lo, 20), [sst], [sst])
            if own:
                tt("dve", qn.v(0, [[256, 2], [32, 8], [1, 32]]), PS(b0, 0, [[256, 2], [32, 8], [1, 32]]),
                   sst.v(20, [[0, 2], [1, 8], [0, 32]]), ALU.mult, R + [sst], [qn])
            tt("dve", qn.v(512, [[64, 2], [32, 2], [1, 32]]), PS(b0, ko, [[64, 2], [32, 2], [1, 32]]),
               sst.v(28, [[0, 2], [1, 2], [0, 32]]), ALU.mult, R + [sst], [qn])
            if own:
                tt("dve", t1.v(0, [[256, 2], [32, 8], [1, 32]]), qn.v(0, [[256, 2], [32, 8], [1, 32]]),
                   GCq[sl].v(i * 64, [[32, 2], [0, 8], [1, 32]]), ALU.mult, [qn, GCq[sl]], [t1])
                tt("dve", t2.v(0, [[256, 2], [32, 8], [1, 32]]), qn.v(256, [[-256, 2], [32, 8], [1, 32]]),
                   GSq[sl].v(i * 64, [[32, 2], [0, 8], [1, 32]]), ALU.mult, [qn, GSq[sl]], [t2])
            tt("dve", t1.v(512, [[64, 2], [32, 2], [1, 32]]), qn.v(512, [[64, 2], [32, 2], [1, 32]]),
               GCk[sl].v(i * 64, [[32, 2], [0, 2], [1, 32]]), ALU.mult, [qn, GCk[sl]], [t1])
            tt("dve", t2.v(512, [[64, 2], [32, 2], [1, 32]]), qn.v(576, [[-64, 2], [32, 2], [1, 32]]),
               GSk[sl].v(i * 64, [[32, 2], [0, 2], [1, 32]]), ALU.mult, [qn, GSk[sl]], [t2])
            if own:
                tt("dve", qr.v(0, [[32, 2], [64, 8], [1, 32]]), t1.v(0, [[256, 2], [32, 8], [1, 32]]),
                   t2.v(0, [[256, 2], [32, 8], [1, 32]]), ALU.add, [t1, t2], [qr])
            tt("dve", qr.v(512, [[32, 2], [64, 2], [1, 32]]), t1.v(512, [[64, 2], [32, 2], [1, 32]]),
               t2.v(512, [[64, 2], [32, 2], [1, 32]]), ALU.add, [t1, t2], [qr])

        def post_part2(blk, i, own, k):
            qr = SCR[k][3]
            b = next_trb()
            if own:
                for g in range(4):
                    tr(PSB(b, g * 128, [[1, 128]]), qr.c(g * 128, (g + 1) * 128), identb.full(), [qr, identb], [PB[b]])
            tr(PSB(b, 512, [[1, 128]]), qr.c(512, 640), identb.full(), [qr, identb], [PB[b]])
            if own:
                act(QT[blk].v(i * 128, [[512, 4], [1, 128]]), PSB(b, 0, [[128, 4], [1, 128]]), AF.Copy,
                    [PB[b]], [QT[blk]])
            act(KT[blk].c(i * 128, (i + 1) * 128), PSB(b, 512, [[1, 128]]), AF.Copy, [PB[b]], [KT[blk]])

        def fm_group(blk, j):
            h_ = hnT[blk % 3]
            b = fmb[fmn[0] % 2]
            fmn[0] += 1
            for c in range(8):
                mm(PS(b, 0, [[1, 512]]), winf.c(c * 1024 + j * 128, c * 1024 + (j + 1) * 128),
                   h_.c(c * 512, (c + 1) * 512), c == 0, c == 7, [h_, winf], [PB[b]])
            if j < 4:
                act(XR[j].c(2 + blk * 512, 2 + (blk + 1) * 512), PS(b, 0, [[1, 512]]), AF.Copy, [PB[b]], [XR[j]])
            else:
                act(GX[j - 4].c(blk * 512, (blk + 1) * 512), PS(b, 0, [[1, 512]]), AF.Copy, [PB[b]], [GX[j - 4]])

        def stage2_tile(blk, i, nxt):
            if nxt is not None:
                stage1_pre(*nxt)
            own = blk < 4
            sl = blk % 2
            h_ = hnT[blk % 3]
            tile = blk * 4 + i
            k = tile % 2
            b0, b1 = qkv_pairs[qkvn[0] % 2]
            qkvn[0] += 1
            if own:
                for c in range(8):
                    mm(PS(b0, 0, [[1, 512]]), h_.c(c * 512 + i * 128, c * 512 + (i + 1) * 128),
                       winq.c(c * 768, c * 768 + 512), c == 0, c == 7, [h_, winq], [PB[b0]])
                for c in range(8):
                    mm(PS(b1, 0, [[1, 256]]), h_.c(c * 512 + i * 128, c * 512 + (i + 1) * 128),
                       winq.c(c * 768 + 512, c * 768 + 768), c == 0, c == 7, [h_, winq], [PB[b1]])
            else:
                for c in range(8):
                    mm(PS(b0, 0, [[1, 256]]), h_.c(c * 512 + i * 128, c * 512 + (i + 1) * 128),
                       winq.c(c * 768 + 512, c * 768 + 768), c == 0, c == 7, [h_, winq], [PB[b0]])
            post_part1(blk, i, b0, own, k)
            if own:
                fm_group(blk, 2 * i)
                fm_group(blk, 2 * i + 1)
            else:
                fm_group(blk, i)
            if nxt is not None:
                stage1_post(*nxt)
            if pending:
                post_part2(*pending.pop())
            pending.append((blk, i, own, k))

        def s1(f_, s_):
            f_(s_ // 4, s_ % 4)

        stage1_tabs(0)
        for s_ in range(5):
            s1(stage1_load, s_)
        wq_stg = []
        for c in range(8):
            xr_t = XR[c // 2]
            stg = T(AR.views[F32], xr_t.start // 4 + 1 + (c % 2) * 768, 768, xr_t.buf)
            wq_stg.append((stg, xr_t))
            dma("sp", stg.full(), w_in_v[:, c, 0:768], f"wq{c}", [], [xr_t])
        for s_ in range(4):
            s1(stage1_pre, s_)
            s1(stage1_post, s_)
            if s_ == 0:
                s1(stage1_load, 5)
        for c in range(8):
            stg, xr_t = wq_stg[c]
            if c % 2 == 0:
                act(winq.c(c * 768, (c + 1) * 768), stg.full(), AF.Copy, [xr_t], [winq])
            else:
                cp("dve", winq.c(c * 768, (c + 1) * 768), stg.full(), [xr_t], [winq])
        s1(stage1_pre, 4)
        s1(stage1_post, 4)
        for blk in range(8):
            if blk + 1 < 8:
                stage1_tabs(blk + 1)
            for i in range(4):
                s_ = 4 * blk + i + 5
                if s_ + 1 < 32:
                    s1(stage1_load, s_ + 1)
                stage2_tile(blk, i, (s_ // 4, s_ % 4) if s_ < 32 else None)
        post_part2(*pending.pop())
        AR.release(mgb, winq, winf, *xt, *xs, *junk, *hnT, *ssqA, *tabC, *tabS, *GCq, *GSq, *GCk, *GSk)
        for k in range(2):
            AR.release(*SCR[k])

        dump_src = {}
        if stop_after == "A":
            dump_src.update({"QT0": QT[0], "KT0": KT[0], "KT5": KT[5], "VX0": VX[0], "XR0": XR[0], "GX0": GX[0],
                             "QT3": QT[3]})


        prs = [(0, 1), (2, 3), (4, 5), (6, 7)]
        prn = [0]

        def next_pair():
            p_ = prs[prn[0] % 4]
            prn[0] += 1
            return p_

        def rev(t_, off, n):
            return t_.v(off + n - 1, [[-1, n]])

        if stop_n >= STAGE["C"]:
            dg = AR.alloc("dg", 40 * 128, BF16)
            for k in range(20):
                ts("dve", dg.c((2 * k) * 128, (2 * k + 1) * 128), identf.full(), misc.c(M_THI + k, M_THI + k + 1),
                   ALU.mult, [identf, misc], [dg])
                ts("dve", dg.c((2 * k + 1) * 128, (2 * k + 2) * 128), identf.full(), misc.c(M_TLO + k, M_TLO + k + 1),
                   ALU.mult, [identf, misc], [dg])
            wbd = AR.alloc("wbd", 16 * 128, BF16)
            dma("pool", wbd.v(0, [[128, 16], [1, 128]]), wbd_d.rearrange("(k p) m -> p k m", p=128), "wbd", [], [wbd])
            for cc in range(4):
                act(GX[cc].full(), GX[cc].full(), AF.Gelu, [GX[cc]], [GX[cc]])
            YL = [AR.alloc(f"YL{i}", TOK, BF16) for i in range(4)]
            YLQ = AR.alloc("YLQ", TOK, BF16)
            ssq_l = AR.alloc("ssq_l", 32, F32)
            xc = [AR.alloc(f"xc{i}", 1024, F32) for i in range(2)]
            xcb = [AR.alloc(f"xcb{i}", 1024, BF16) for i in range(2)]
            sets = [[AR.alloc(f"lr{i}", 1024, F32), AR.alloc(f"la{i}", 1024, F32), AR.alloc(f"li{i}", 1024, F32)]
                    for i in range(4)]
            setn = [0]
            HF = [AR.alloc(f"HF{i}", 1024, F32) for i in range(2)]
            HB = AR.alloc("HB", 2048, F32)

            def conv(cc, tb, slot):
                pr = next_pair()
                for half in range(2):
                    n = 0
                    for j in range(5):
                        for hl in range(2):
                            k = ((cc * 5 + j) * 2 + hl) * 128
                            o = tb * 1024 + half * 512 + j
                            mm(PS(pr[half], 0, [[1, 512]]), dg.c(k, k + 128), XR[cc].c(o, o + 512),
                               n == 0, n == 9, [dg, XR[cc]], [PB[pr[half]]])
                            n += 1
                act(xc[slot].full(), PS(pr[0], 0, [[1, 1024]]), AF.Identity, [PB[pr[0]], PB[pr[1]], prm], [xc[slot]],
                    bias=prm.c(P_CONVB + cc, P_CONVB + cc + 1))
                cp("dve", xcb[slot].full(), xc[slot].full(), [xc[slot]], [xcb[slot]])

            def gate_pre(cc, e, slot):
                pa = next_pair()
                px = next_pair()
                for ty, pp in ((0, pa), (1, px)):
                    k = ((e * 2 + ty) * 4 + cc) * 128
                    for half in range(2):
                        mm(PS(pp[half], 0, [[1, 512]]), wbd.c(k, k + 128), xcb[slot].c(half * 512, half * 512 + 512),
                           True, True, [wbd, xcb[slot]], [PB[pp[half]]])
                st_ = sets[setn[0] % 4]
                setn[0] += 1
                Rr, Aa, Ii = st_
                col = e * 4 + cc
                act(Rr.full(), PS(pa[0], 0, [[1, 1024]]), AF.Tanh, [PB[pa[0]], PB[pa[1]], misc], [Rr],
                    bias=misc.c(M_HBA + col, M_HBA + col + 1), scale=0.5)
                act(Ii.full(), PS(px[0], 0, [[1, 1024]]), AF.Tanh, [PB[px[0]], PB[px[1]], misc], [Ii],
                    bias=misc.c(M_HBX + col, M_HBX + col + 1), scale=0.5)
                act(Aa.full(), Rr.full(), AF.Exp, [Rr, misc], [Aa], scale=misc.c(M_HCL + col, M_HCL + col + 1),
                    bias=misc.c(M_HCL + col, M_HCL + col + 1))
                stt(Ii.full(), Ii.full(), 1.0, xc[slot].full(), ALU.add, ALU.mult, [Ii, xc[slot]], [Ii])
                tt("dve", Rr.full(), Aa.full(), Aa.full(), ALU.mult, [Aa], [Rr])
                return st_

            def gate_sqrt(st_):
                Rr = st_[0]
                act(Rr.full(), Rr.full(), AF.Sqrt, [Rr], [Rr], scale=-1.0, bias=1.0)

            def gate_scan(e, st_, hout, ho, init, init_rd):
                Rr, Aa, Ii = st_
                stt(Ii.full(), Ii.full(), 0.5, Rr.full(), ALU.mult, ALU.mult, [Ii, Rr], [Ii])
                if e == 0:
                    scan(hout.c(ho, ho + 1024), Aa.full(), Ii.full(), init, [Aa, Ii] + init_rd, [hout])
                else:
                    scan(rev(hout, ho, 1024), rev(Aa, 0, 1024), rev(Ii, 0, 1024), init, [Aa, Ii] + init_rd, [hout])

            def combine(cc, on_act=True):
                for tbo in range(2):
                    tt("dve", HF[tbo].full(), HF[tbo].full(), HB.c(tbo * 1024, tbo * 1024 + 1024), ALU.add,
                       [HF[tbo], HB], [HF[tbo]])
                    tt("dve", HF[tbo].full(), HF[tbo].full(), GX[cc].c(tbo * 1024, tbo * 1024 + 1024), ALU.mult,
                       [HF[tbo], GX[cc]], [HF[tbo]])
                    if on_act:
                        act(YLQ.c(tbo * 1024, tbo * 1024 + 1024), HF[tbo].full(), AF.Square, [HF[tbo]], [YLQ])
                        act(YL[cc].c(tbo * 1024, tbo * 1024 + 1024), HF[tbo].full(), AF.Copy, [HF[tbo], prm], [YL[cc]],
                            scale=prm.c(P_LG + cc, P_LG + cc + 1))
                    else:
                        tt("dve", YLQ.c(tbo * 1024, tbo * 1024 + 1024), HF[tbo].full(), HF[tbo].full(), ALU.mult,
                           [HF[tbo]], [YLQ])
                        ts("dve", YL[cc].c(tbo * 1024, tbo * 1024 + 1024), HF[tbo].full(),
                           prm.c(P_LG + cc, P_LG + cc + 1), ALU.mult, [HF[tbo], prm], [YL[cc]])

            def combine_pe(cc, pr=None):
                if pr is None:
                    pr = next_pair()
                for i in range(16):
                    mm(PS(pr[0], i, [[1, 1]]), YLQ.c(i * 128, (i + 1) * 128), onesb.c(0, 1), True, True,
                       [YLQ, onesb], [PB[pr[0]]])
                if cc == 0:
                    cp("dve", ssq_l.c(0, 16), PS(pr[0], 0, [[1, 16]]), [PB[pr[0]]], [ssq_l])
                else:
                    tt("dve", ssq_l.c(0, 16), ssq_l.c(0, 16), PS(pr[0], 0, [[1, 16]]), ALU.add, [ssq_l, PB[pr[0]]], [ssq_l])

            conv(0, 3, 0)
            conv(0, 2, 1)
            for cc in range(4):
                s3 = gate_pre(cc, 1, 0)
                s2 = gate_pre(cc, 1, 1)
                conv(cc, 1, 0)
                conv(cc, 0, 1)
                if cc >= 1:
                    combine(cc - 1)
                gate_sqrt(s3)
                gate_sqrt(s2)
                gate_scan(1, s3, HF[1], 0, 0.0, [])
                gate_scan(1, s2, HF[0], 0, HF[1].c(0, 1), [HF[1]])
                s1 = gate_pre(cc, 1, 0)
                s0 = gate_pre(cc, 1, 1)
                if cc >= 1:
                    combine_pe(cc - 1)
                gate_sqrt(s1)
                gate_sqrt(s0)
                gate_scan(1, s1, HB, 1024, HF[0].c(0, 1), [HF[0]])
                gate_scan(1, s0, HB, 0, HB.c(1024, 1025), [HB])
                f0 = gate_pre(cc, 0, 1)
                f1 = gate_pre(cc, 0, 0)
                if cc + 1 < 4:
                    conv(cc + 1, 3, 0)
                    conv(cc + 1, 2, 1)
                gate_sqrt(f0)
                gate_sqrt(f1)
                gate_scan(0, f0, HF[0], 0, 0.0, [])
                gate_scan(0, f1, HF[1], 0, HF[0].c(1023, 1024), [HF[0]])
            defer_c3 = stop_n >= STAGE["G"]
            if defer_c3:
                AR.release(*xc, *xcb, *XR, GX[0], GX[1], GX[2], dg, wbd, thb)
            else:
                combine(3)
                combine_pe(3)
                AR.release(YLQ, *xc, *xcb, *HF, HB, *XR, *GX, dg, wbd, thb)
            for st_ in sets:
                AR.release(*st_)
            if stop_n == STAGE["C"]:
                dump_src.update({"YL0": YL[0], "YL3": YL[3], "ssq_l": ssq_l})

        if stop_n >= STAGE["B"]:
            YAP = [AR.alloc(f"YAP{i}", TOK, BF16) for i in range(4)]
            ssq_a = AR.alloc("ssq_a", 32, F32)
            PT = [AR.alloc(f"PT{i}", 1024, BF16) for i in range(3)]
            rd = [AR.alloc(f"rd{i}", 512, F32) for i in range(2)]
            bcs = [AR.alloc(f"bcs{i}", 512, F32) for i in range(2)]
            y0 = [AR.alloc(f"y0{i}", 512, F32) for i in range(2)]
            ysq = AR.alloc("ysq", 4 * 512, BF16)
            if stop_n >= STAGE["G"]:
                WoA = AR.alloc("WoA", 4 * 1024, BF16)
                WoL = AR.alloc("WoL", 4 * 1024, BF16)
                dma("pool", WoA.v(0, [[1024, 4], [1, 1024]]),
                    w_out[0:512, :].rearrange("(g q) n -> q g n", q=128), "wo", [], [WoA])
                dma("pool", WoL.v(0, [[1024, 4], [1, 1024]]),
                    w_out[512:1024, :].rearrange("(c p) n -> p c n", p=128), "wo", [], [WoL])
                H = [AR.alloc(f"H{i}", D, F32) for i in range(16)]
                for i in range(16):
                    dma("sp", H[i].full(), x_own[i * 128:(i + 1) * 128, :], f"hx{i // 4}", [], [H[i]])
            sprs = [(0, 1), (2, 3)]
            uprs = [(4, 5), (6, 7)]
            ptn = [0]
            scale = 64 ** -0.5

            def qk(n, sp):
                qb, g, kt = n // 128, (n // 32) % 4, n % 32
                kT = KT[kt // 4]
                ko = (kt % 4) * 128
                for h2 in range(2):
                    mm(PS(sp[h2], 0, [[1, 512]]), kT.c(ko, ko + 128, p0=64 * h2, np_=64),
                       QT[qb].c(g * 512, (g + 1) * 512, p0=64 * h2, np_=64), True, True, [kT, QT[qb]], [PB[sp[h2]]])

            def epi1(up):
                for h2 in range(2):
                    dr = 64 if h2 == 0 else 0
                    p0 = 64 * h2
                    recip(rd[h2].c(0, 512, p0=dr, np_=1), PS(up[h2], 0, [[1, 512]], p0=dr, np_=1), [PB[up[h2]]], [rd[h2]])
                    src = bass.AP(rd[h2].base.tensor, dr * rd[h2].pstep + rd[h2].off, [[rd[h2].pstep, 1], [0, 64], [1, 512]])
                    dma("sp", bcs[h2].c(0, 512, p0=p0, np_=64), src, f"bc{h2}", [rd[h2]], [bcs[h2]])

            def epi2(qb, g, up, bp):
                for h2 in range(2):
                    p0 = 64 * h2
                    tt("dve", y0[h2].c(0, 512, p0=p0, np_=64), PS(up[h2], 0, [[1, 512]], p0=p0, np_=64),
                       bcs[h2].c(0, 512, p0=p0, np_=64), ALU.mult, [PB[up[h2]], bcs[h2]], [y0[h2]])
                    tt("dve", ysq.c(g * 512, (g + 1) * 512, p0=p0, np_=64), y0[h2].c(0, 512, p0=p0, np_=64),
                       y0[h2].c(0, 512, p0=p0, np_=64), ALU.mult, [y0[h2]], [ysq])
                    ts("dve", YAP[g].c(qb * 512, (qb + 1) * 512, p0=p0, np_=64), y0[h2].c(0, 512, p0=p0, np_=64),
                       prm.c(P_AG + g, P_AG + g + 1, p0=p0, np_=64), ALU.mult, [y0[h2], prm], [YAP[g]])

            def epi3(qb, bp):
                for i in range(4):
                    for g_ in range(4):
                        mm(PS(bp[0], i, [[1, 1]]), ysq.c(g_ * 512 + i * 128, g_ * 512 + (i + 1) * 128),
                           onesb.c(0, 1), g_ == 0, g_ == 3, [ysq, onesb], [PB[bp[0]]])
                cp("dve", ssq_a.c(qb * 4, qb * 4 + 4), PS(bp[0], 0, [[1, 4]]), [PB[bp[0]]], [ssq_a])

            def d_tile(i, pa, pl):
                for half in range(2):
                    for g_ in range(4):
                        mm(PS(pa[half], 0, [[1, 512]]), YAP[g_].c(i * 128, (i + 1) * 128),
                           WoA.c(g_ * 1024 + half * 512, g_ * 1024 + half * 512 + 512), g_ == 0, g_ == 3,
                           [YAP[g_], WoA], [PB[pa[half]]])
                for half in range(2):
                    for cc in range(4):
                        mm(PS(pl[half], 0, [[1, 512]]), YL[cc].c(i * 128, (i + 1) * 128),
                           WoL.c(cc * 1024 + half * 512, cc * 1024 + half * 512 + 512), cc == 0, cc == 3,
                           [YL[cc], WoL], [PB[pl[half]]])
                stt(H[i].full(), PS(pa[0], 0, [[1, 1024]]), ssq_a.c(16 + i, 17 + i), H[i].full(), ALU.mult, ALU.add,
                    [PB[pa[0]], PB[pa[1]], ssq_a, H[i]], [H[i]])
                stt(H[i].full(), PS(pl[0], 0, [[1, 1024]]), ssq_l.c(16 + i, 17 + i), H[i].full(), ALU.mult, ALU.add,
                    [PB[pl[0]], PB[pl[1]], ssq_l, H[i]], [H[i]])

            NIT = 512
            todo = {}
            if stop_n >= STAGE["G"]:
                todo[3] = [lambda bp: combine(3, on_act=False)]
                todo[12] = [lambda bp: (combine_pe(3, uprs[1]), AR.release(YLQ, *HF, HB, GX[3]))]
            qk(0, sprs[0])
            qk(1, sprs[1])
            for n in range(NIT):
                qb, g, kt = n // 128, (n // 32) % 4, n % 32
                up = uprs[(n // 32) % 2]
                sp = sprs[n % 2]
                pt = PT[ptn[0] % 3]
                ptn[0] += 1
                act(pt.full(), PS(sp[0], 0, [[1, 1024]]), AF.Exp, [PB[sp[0]], PB[sp[1]], misc], [pt],
                    bias=misc.c(M_NB, M_NB + 1), scale=scale)
                if n + 2 < NIT:
                    qk(n + 2, sp)
                vx = VX[kt // 4]
                vo = (kt % 4) * 256
                for h2 in range(2):
                    mm(PS(up[h2], 0, [[1, 512]]), vx.c(vo + 128 * h2, vo + 128 * h2 + 128),
                       pt.c(512 * h2, 512 * h2 + 512), kt == 0, kt == 31, [vx, pt], [PB[up[h2]]])
                for f_ in todo.pop(n, []):
                    f_(None)
                if kt == 31:
                    epi1(up)
                    last = (n + 1 >= NIT)
                    d2 = n if last else n + 8
                    todo.setdefault(d2, []).append(lambda bp, qb=qb, g=g, up=up: epi2(qb, g, up, bp))
                    if g == 3:
                        d3 = n if last else n + 14
                        todo.setdefault(d3, []).append(lambda bp, qb=qb, up=up: epi3(qb, up))
                    if last:
                        if stop_n >= STAGE["G"]:
                            act(ssq_a.c(16, 28), ssq_a.c(0, 12), AF.Sqrt, [ssq_a], [ssq_a], scale=1.0 / 512, bias=EPS)
                            recip(ssq_a.c(16, 28), ssq_a.c(16, 28), [ssq_a], [ssq_a])
                            act(ssq_l.c(16, 32), ssq_l.c(0, 16), AF.Sqrt, [ssq_l], [ssq_l], scale=1.0 / 512, bias=EPS)
                            recip(ssq_l.c(16, 32), ssq_l.c(16, 32), [ssq_l], [ssq_l])
                            free_u = uprs[0] if up is uprs[1] else uprs[1]
                            d_tile(0, sprs[0], sprs[1])
                            d_tile(1, free_u, sprs[0])
                        for f_ in todo.pop(n, []):
                            f_(None)
            assert not todo
            AR.release(*PT, *rd, *bcs, *y0, ysq, *QT, *KT, *VX)
            if stop_n == STAGE["B"]:
                dump_src.update({"YA0": YAP[0], "YA5": YAP[2], "ssq_a": ssq_a})


        final_keys = []
        if stop_n >= STAGE["G"]:
            def next_bank():
                return next_pair()[0]

            ssq_m = AR.alloc("ssq_m", 48, F32)
            WU = [AR.alloc(f"WU{i}", 8 * 1024, BF16) for i in range(1)]
            WD = [AR.alloc(f"WD{i}", 8 * 1024, BF16) for i in range(1)]

            def load_w(qd):
                sl = qd % 2
                dma("pool", WU[sl].v(0, [[1024, 8], [1, 1024]]),
                    w_up[:, qd * 1024:(qd + 1) * 1024].rearrange("(c p) n -> p c n", p=128), f"wu{sl}", [], [WU[sl]])
                dma("pool", WD[sl].v(0, [[1024, 8], [1, 1024]]),
                    w_down[qd * 1024:(qd + 1) * 1024, :].rearrange("(j p) n -> p j n", p=128), f"wd{sl}", [], [WD[sl]])

            load_w(0)
            PBT = AR.alloc("PBT", 16 * 256, BF16)
            PTT = [AR.alloc(f"PTT{i}", 2 * 512, BF16) for i in range(4)]
            dma("pool", PBT.v(0, [[256, 16], [1, 256]]), p_own.rearrange("(t p) f -> p t f", p=128), "pbt", [], [PBT])
            junk2 = [AR.alloc(f"junk2_{i}", D, BF16) for i in range(2)]
            j2n = [0]

            def sq_accum(src, dst_t, col):
                jk = junk2[j2n[0] % 2]
                j2n[0] += 1
                act(jk.full(), src.full(), AF.Square, [src], [jk, dst_t], accum=dst_t.c(col, col + 1))
            act(ssq_a.c(28, 32), ssq_a.c(12, 16), AF.Sqrt, [ssq_a], [ssq_a], scale=1.0 / 512, bias=EPS)
            recip(ssq_a.c(28, 32), ssq_a.c(28, 32), [ssq_a], [ssq_a])
            HN = []
            xs2 = []

            def nt_xs(ssq_t, i):
                x2 = xs2[i % 2]
                act(x2.full(), H[i].full(), AF.Copy, [H[i], ssq_t], [x2], scale=ssq_t.c(32 + i, 33 + i))

            def nt_tr(i):
                x2 = xs2[i % 2]
                b = next_bank()
                for c in range(8):
                    tr(PSB(b, c * 128, [[1, 128]]), x2.c(c * 128, (c + 1) * 128), identb.full(), [x2, identb], [PB[b]])
                return b

            def nt_ev(gcol, tb, i4, b):
                tt("dve", HN[tb].v(i4 * 128, [[512, 8], [1, 128]]), PSB(b, 0, [[128, 8], [1, 128]]),
                   prm.v(gcol, [[1, 8], [0, 128]]), ALU.mult, [PB[b], prm], [HN[tb]])

            def rstd_pow(t_, src0, mid0, dst0, n_):
                ts("dve", t_.c(mid0, mid0 + n_), t_.c(src0, src0 + n_), 1.0 / D, ALU.mult, [t_], [t_], s2=EPS, op1=ALU.add)
                tt("pool", t_.c(dst0, dst0 + n_), t_.c(mid0, mid0 + n_), mhalf.c(0, n_), ALU.pow, [t_, mhalf], [t_])

            def nt_pre(ssq_t, tb):
                c0 = tb * 4
                rstd_pow(ssq_t, c0, 16 + c0, 32 + c0, 4)
                nt_xs(ssq_t, c0)
                nt_xs(ssq_t, c0 + 1)

            def nt_postA(gcol, ssq_t, tb):
                c0 = tb * 4
                b0 = nt_tr(c0)
                b1 = nt_tr(c0 + 1)
                nt_ev(gcol, tb, 0, b0)
                nt_xs(ssq_t, c0 + 2)
                nt_ev(gcol, tb, 1, b1)
                nt_xs(ssq_t, c0 + 3)

            def nt_postB(gcol, ssq_t, tb):
                c0 = tb * 4
                b2 = nt_tr(c0 + 2)
                b3 = nt_tr(c0 + 3)
                nt_ev(gcol, tb, 2, b2)
                nt_ev(gcol, tb, 3, b3)

            def nt_post(gcol, ssq_t, tb):
                nt_postA(gcol, ssq_t, tb)
                nt_postB(gcol, ssq_t, tb)

            xs2 += [AR.alloc(f"xs2_{i}", D, BF16) for i in range(2)]

            def p_transposes():
                for i_ in range(16):
                    tb_, i4_ = i_ // 4, i_ % 4
                    b = next_bank()
                    for c2 in range(2):
                        tr(PSB(b, c2 * 128, [[1, 128]]), PBT.c(i_ * 256 + c2 * 128, i_ * 256 + (c2 + 1) * 128),
                           identb.full(), [PBT, identb], [PB[b]])
                    cp("dve", PTT[tb_].v(i4_ * 128, [[512, 2], [1, 128]]), PSB(b, 0, [[128, 2], [1, 128]]),
                       [PB[b]], [PTT[tb_]])
                AR.release(PBT)

            for i in range(16):
                if i == 4:
                    nt_pre(ssq_m, 0)
                if i == 9:
                    p_transposes()
                if i >= 2:
                    d_tile(i, next_pair(), next_pair())
                sq_accum(H[i], ssq_m, i)
            AR.release(*YAP, *YL, WoA, WoL)
            HN += [AR.alloc(f"HN{i}", 8 * 512, BF16) for i in range(4)]

            nt_post(P_MLP, ssq_m, 0)
            nt_pre(ssq_m, 1)
            nt_postA(P_MLP, ssq_m, 1)
            WU.append(AR.alloc("WU1", 8 * 1024, BF16))
            WD.append(AR.alloc("WD1", 8 * 1024, BF16))
            load_w(1)
            AT = [[AR.alloc(f"AT{i}_{h}", 4 * 512, BF16) for h in range(2)] for i in range(2)]
            rl = [AR.alloc(f"rl{i}", 512, F32) for i in range(2)]
            rln = [0]
            steps = [(qd, tb) for qd in range(4) for tb in range(4)]

            def up(n):
                qd, tb = steps[n]
                sl = qd % 2
                at = AT[n % 2]
                for j in range(8):
                    b = next_bank()
                    for c in range(8):
                        mm(PS(b, 0, [[1, 512]]), WU[sl].c(c * 1024 + j * 128, c * 1024 + (j + 1) * 128),
                           HN[tb].c(c * 512, (c + 1) * 512), c == 0, c == 7, [WU[sl], HN[tb]], [PB[b]])
                    r_ = rl[rln[0] % 2]
                    rln[0] += 1
                    act(r_.full(), PS(b, 0, [[1, 512]]), AF.Relu, [PB[b]], [r_])
                    tt("dve", at[j // 4].c((j % 4) * 512, (j % 4 + 1) * 512), r_.full(), r_.full(), ALU.mult, [r_], [at[j // 4]])

            def down(n):
                qd, tb = steps[n]
                sl = qd % 2
                at = AT[n % 2]
                for i4 in range(4):
                    i = tb * 4 + i4
                    pr = next_pair()
                    for half in range(2):
                        for j in range(8):
                            mm(PS(pr[half], 0, [[1, 512]]),
                               at[j // 4].c((j % 4) * 512 + i4 * 128, (j % 4) * 512 + (i4 + 1) * 128),
                               WD[sl].c(j * 1024 + half * 512, j * 1024 + half * 512 + 512), j == 0, j == 7,
                               [at[j // 4], WD[sl]], [PB[pr[half]]])
                    tt("dve", H[i].full(), H[i].full(), PS(pr[0], 0, [[1, 1024]]), ALU.add,
                       [H[i], PB[pr[0]], PB[pr[1]]], [H[i]])

            ssq_p = AR.alloc("ssq_p", 48, F32)
            fs = AR.alloc("fs", 48, F32)
            Fst = {}

            def ple_setup():
                Fst["WG"] = AR.alloc("WG", 8 * 1024, BF16)
                Fst["WP"] = AR.alloc("WP", 2 * 1024, BF16)
                Fst["fg"] = AR.alloc("fg", D, F32)
                Fst["gate"] = [AR.alloc("gate0", D, F32)]
                Fst["ot"] = [AR.alloc(f"ot{i}", D, F32) for i in range(2)]
                dma("pool", Fst["WG"].v(0, [[1024, 8], [1, 1024]]), w_gate.rearrange("(c p) n -> p c n", p=128),
                    "wg", [], [Fst["WG"]])
                dma("pool", Fst["WP"].v(0, [[1024, 2], [1, 1024]]), w_proj.rearrange("(c p) n -> p c n", p=128),
                    "wg", [], [Fst["WP"]])
                dma("sp", Fst["fg"].full(), fing.partition_broadcast(128), "fg", [], [Fst["fg"]])

            def ple_block(tb):
                WG, WP, fg = Fst["WG"], Fst["WP"], Fst["fg"]
                if tb < 3:
                    nt_postB(P_PLE, ssq_p, tb)
                if tb + 1 < 4:
                    nt_pre(ssq_p, tb + 1)
                if tb == 2:
                    nt_postA(P_PLE, ssq_p, 3)
                    nt_postB(P_PLE, ssq_p, 3)
                for i4 in range(4):
                    i = tb * 4 + i4
                    pg = next_pair()
                    pp = next_pair()
                    for half in range(2):
                        for c in range(8):
                            mm(PS(pg[half], 0, [[1, 512]]), HN[tb].c(c * 512 + i4 * 128, c * 512 + (i4 + 1) * 128),
                               WG.c(c * 1024 + half * 512, c * 1024 + half * 512 + 512), c == 0, c == 7,
                               [HN[tb], WG], [PB[pg[half]]])
                    for half in range(2):
                        for c2 in range(2):
                            mm(PS(pp[half], 0, [[1, 512]]), PTT[tb].c(c2 * 512 + i4 * 128, c2 * 512 + (i4 + 1) * 128),
                               WP.c(c2 * 1024 + half * 512, c2 * 1024 + half * 512 + 512), c2 == 0, c2 == 1,
                               [PTT[tb], WP], [PB[pp[half]]])
                    gt = Fst["gate"][0]
                    act(gt.full(), PS(pg[0], 0, [[1, 1024]]), AF.Tanh, [PB[pg[0]], PB[pg[1]]], [gt], scale=0.5)
                    stt(gt.full(), gt.full(), 1.0, PS(pp[0], 0, [[1, 1024]]), ALU.add, ALU.mult,
                        [gt, PB[pp[0]], PB[pp[1]]], [gt])
                    stt(H[i].full(), gt.full(), 0.5, H[i].full(), ALU.mult, ALU.add, [gt, H[i]], [H[i]])
                    sq_accum(H[i], fs, i)
                    rstd_pow(fs, i, 16 + i, 32 + i, 1)
                    o_ = Fst["ot"][i % 2]
                    stt(o_.full(), H[i].full(), fs.c(32 + i, 33 + i), fg.full(), ALU.mult, ALU.mult, [H[i], fs, fg], [o_])
                    dma("sp", out_d[i * 128:(i + 1) * 128, :], o_.full(), f"out{i % 2}", [o_], [])

            up(0)
            nt_postB(P_MLP, ssq_m, 1)
            nt_pre(ssq_m, 2)
            for n in range(16):
                qd, tb = steps[n]
                if qd == 3 and tb == 1:
                    nt_pre(ssq_p, 0)
                if n + 1 < 16:
                    up(n + 1)
                if n in (0, 1):
                    nt_postA(P_MLP, ssq_m, n + 2)
                if qd == 3 and tb >= 1:
                    nt_postA(P_PLE, ssq_p, tb - 1)
                down(n)
                if n in (0, 1):
                    nt_postB(P_MLP, ssq_m, n + 2)
                    if n == 0:
                        nt_pre(ssq_m, 3)
                if tb == 3 and qd + 2 < 4:
                    load_w(qd + 2)
                if qd == 3:
                    if tb == 0:
                        AR.release(WU[0], WD[0])
                        ple_setup()
                    for i4 in range(4):
                        sq_accum(H[tb * 4 + i4], ssq_p, tb * 4 + i4)
                    if tb >= 1:
                        ple_block(tb - 1)
            ple_block(3)
            final_keys += ["out0", "out1"]

        if dumps:
            stg = AR.alloc("dump_stage", 4100, F32)
            for n_, (nm, ncols) in enumerate(dumps):
                src = dump_src[nm]
                cp("dve", stg.c(0, ncols), src.c(0, ncols), [src], [stg])
                dma("sp", dump_d[nm][:, :], stg.c(0, ncols), "dump", [stg], [])
            final_keys.append("dump")

        print("arena peak bytes:", AR.peak, "ops:", len(S.ops))
        S.finalize()
        keys = sorted(S.dma_count.keys())
        with ExitStack() as es2:
            engsem = {e: es2.enter_context(nc.semaphore("sem_" + e)) for e in Sched.ENGS}
            dmasem = {k: es2.enter_context(nc.semaphore("dma_" + k)) for k in keys}
            block = es2.enter_context(nc.Block())
            S.emit(nc, block, engsem, dmasem, final_keys)
    return nc


def _rope_tables_local(t):
    rows = S_FULL // 64
    pos = np.arange(S_FULL)
    row = (pos // 64).astype(np.float32)
    col = (pos % 64).astype(np.float32)
    inv = (np.float32(10000.0) ** (-np.arange(16, dtype=np.float32) / np.float32(16))).astype(np.float32)
    ang_r = row[:, None] * inv[None, :]
    ang_c = col[:, None] * inv[None, :]
    cs = np.stack([np.cos(ang_r), np.cos(ang_c)], axis=1).astype(np.float32)
    sn = np.stack([np.sin(ang_r), np.sin(ang_c)], axis=1).astype(np.float32)
    C = np.concatenate([cs.reshape(S_FULL, 32), cs.reshape(S_FULL, 32)], axis=1)
    Sg = np.concatenate([-sn.reshape(S_FULL, 32), sn.reshape(S_FULL, 32)], axis=1)
    if t == 0:
        order = np.arange(S_FULL)
    else:
        order = np.concatenate([np.arange(4095, 2047, -1), np.arange(2047, -1, -1)])
    return np.ascontiguousarray(C[order]), np.ascontiguousarray(Sg[order])


def _perm_sai(v64):
    v = np.asarray(v64).reshape(2, 2, 16)
    perm = v.transpose(1, 0, 2).reshape(64)
    swap = v[:, ::-1, :].transpose(1, 0, 2).reshape(64)
    return perm, swap


def prep_inputs(inp):
    f = lambda a: np.ascontiguousarray(np.asarray(a, dtype=np.float32))
    x = f(inp["x"])
    p = f(inp["p"])[0]
    w_in = f(inp["w_in"])[0]
    qcols = np.zeros(512, np.int64)
    for s in range(2):
        for hp in range(8):
            head = (hp % 2) * 4 + hp // 2
            for a in range(2):
                for i in range(16):
                    qcols[s * 256 + hp * 32 + a * 16 + i] = head * 64 + a * 32 + s * 16 + i
    kcols = np.zeros(128, np.int64)
    for s in range(2):
        for h in range(2):
            for a in range(2):
                for i in range(16):
                    kcols[s * 64 + h * 32 + a * 16 + i] = 512 + h * 64 + a * 32 + s * 16 + i
    cols = np.concatenate([qcols, kcols, np.arange(640, D_IN)])
    w_in_p = np.ascontiguousarray(w_in[:, cols])
    gqp, gqs = _perm_sai(f(inp["q_norm"])[0])
    gkp, gks = _perm_sai(f(inp["k_norm"])[0])
    gqk = np.concatenate([gqp, gqs, gkp, gks])[None, :].astype(np.float32)

    def pc(v, n):
        return np.asarray(v, np.float32).reshape(n, 128).T

    conv_w = f(inp["conv_w"])[0]
    attn_g = f(inp["attn_out_norm"])[0].reshape(8, 64)
    ag = np.zeros((128, 8), np.float32)
    for hh in range(8):
        head = (hh % 2) * 4 + hh // 2
        ag[(hh % 2) * 64:(hh % 2) * 64 + 64, hh // 2] = attn_g[head]
    w_out = f(inp["w_out"])[0]
    wo_rows = np.concatenate([np.arange(((hh % 2) * 4 + hh // 2) * 64, ((hh % 2) * 4 + hh // 2) * 64 + 64)
                              for hh in range(8)] + [np.arange(512, 1024)])
    w_out_p = np.ascontiguousarray(w_out[wo_rows])
    common = {
        "w_in": w_in_p, "gqk": gqk, "w_out": w_out_p, "w_up": f(inp["w_up"])[0], "w_down": f(inp["w_down"])[0],
        "w_gate": f(inp["w_ple_gate"])[0], "w_proj": f(inp["w_ple_proj"])[0], "fing": f(inp["final_norm"])[None, :], "mixg_row": f(inp["mix_norm"])[0][None, :],
    }
    rope = {t: _rope_tables_local(t) for t in range(2)}
    per_t = {}
    for t in range(2):
        dirs = [0, 1] if t == 0 else [1, 0]
        taps5 = np.zeros((5, 512), np.float32)
        if t == 0:
            taps5[0:4] = conv_w
        else:
            taps5[1:5] = conv_w[::-1]
        prm = np.zeros((128, 96), np.float32)
        prm[:, 0:8] = pc(f(inp["mix_norm"])[0], 8)
        prm[:, 8:16] = pc(f(inp["mlp_norm"])[0], 8)
        prm[:, 16:24] = pc(f(inp["ple_norm"])[0], 8)
        for cc in range(4):
            for j in range(5):
                prm[:, 24 + cc * 5 + j] = taps5[j, cc * 128:(cc + 1) * 128]
        prm[:, 44:48] = pc(f(inp["conv_b"])[0], 4)
        for e in range(2):
            prm[:, 48 + e * 4:48 + e * 4 + 4] = pc(f(inp["lru_ba"])[0, dirs[e]], 4)
            prm[:, 56 + e * 4:56 + e * 4 + 4] = pc(f(inp["lru_bx"])[0, dirs[e]], 4)
            prm[:, 64 + e * 4:64 + e * 4 + 4] = pc(f(inp["lru_lambda"])[0, dirs[e]], 4)
        prm[:, 72:80] = ag
        prm[:, 80:84] = pc(f(inp["lru_out_norm"])[0], 4)
        wa = f(inp["lru_wa"])[0]
        wx = f(inp["lru_wx"])[0]
        wbd = np.zeros((16, 128, 128), np.float32)
        for e in range(2):
            for ty, wsrc in enumerate((wa, wx)):
                for cc in range(4):
                    k = (e * 2 + ty) * 4 + cc
                    wbd[k, 0:64, 0:64] = wsrc[dirs[e], 2 * cc]
                    wbd[k, 64:128, 64:128] = wsrc[dirs[e], 2 * cc + 1]
        per_t[t] = {"prm": prm, "wbd": wbd.reshape(16 * 128, 128), "ropeC": rope[t][0], "ropeS": rope[t][1]}
    in_maps = []
    for c in range(N_CORES):
        b, t = c // 2, c % 2
        if t == 0:
            xo, xh, po = x[b, :TOK], x[b, TOK:], p[b, :TOK]
        else:
            xo, xh, po = x[b, TOK:][::-1], x[b, :TOK][::-1], p[b, TOK:][::-1]
        m = dict(common)
        m.update(per_t[t])
        m["x_own"] = np.ascontiguousarray(xo)
        m["x_oth"] = np.ascontiguousarray(xh)
        m["p_own"] = np.ascontiguousarray(po)
        in_maps.append(m)
    return in_maps


_NC_CACHE = {}


def kernel(**inputs):
    in_maps = prep_inputs(inputs)
    if "nc" not in _NC_CACHE:
        _NC_CACHE["nc"] = build_program()
    nc = _NC_CACHE["nc"]
    res = run_bass_kernel_spmd(nc, in_maps, core_ids=list(range(N_CORES)))
    out = np.zeros((4, S_FULL, D), np.float32)
    for c in range(N_CORES):
        b, t = c // 2, c % 2
        o = np.asarray(res.results[c]["out"], np.float32)
        if t == 0:
            out[b, :TOK] = o
        else:
            out[b, TOK:] = o[::-1]
    return out
```
